# Optimizing a Trainium2 kernel written in Bass

```python
import jax, jax.numpy as jnp
from jax import lax
import numpy as np

D_MODEL = 1024
BATCH = 16
SEQ = 4096
DEPTH = 2

HEAD_DIM = 64
ATTN_WIDTH = D_MODEL // 2
N_Q_HEADS = ATTN_WIDTH // HEAD_DIM
N_KV_HEADS = max(N_Q_HEADS // 4, 1)
Q_PER_KV = N_Q_HEADS // N_KV_HEADS
KV_WIDTH = N_KV_HEADS * HEAD_DIM
WINDOW = 128
BLOCK = 128

CHUNK = 128
SGU_GROUP_DIM = 128
SGU_WIDTH = D_MODEL // 2
SGU_GROUPS = SGU_WIDTH // SGU_GROUP_DIM

ALPHA = (2.0 * DEPTH) ** 0.25
BETA = (8.0 * DEPTH) ** -0.25
LN_EPS = 1e-5

SPLITS = (ATTN_WIDTH, KV_WIDTH, KV_WIDTH, ATTN_WIDTH,
          SGU_WIDTH, SGU_WIDTH, SGU_WIDTH, D_MODEL, D_MODEL)
IN_COLS = sum(SPLITS)

kernel_name = "hybrid_swa_sink_sgu_gated_deepnorm"


def _layernorm(x, g, b):
    xf = x.astype(jnp.float32)
    mu = jnp.mean(xf, axis=-1, keepdims=True)
    var = jnp.mean(jnp.square(xf - mu), axis=-1, keepdims=True)
    y = (xf - mu) * lax.rsqrt(var + LN_EPS) * g.astype(jnp.float32) + b.astype(jnp.float32)
    return y.astype(x.dtype)


def _split_cols(h):
    parts, start = [], 0
    for w in SPLITS:
        parts.append(h[..., start:start + w])
        start += w
    return parts


def _swa_sinks(q, k, v, sinks):
    B, S = q.shape[0], q.shape[1]
    nb = S // BLOCK
    qb = q.reshape(B, nb, BLOCK, N_KV_HEADS, Q_PER_KV, HEAD_DIM)
    kb = k.reshape(B, nb, BLOCK, N_KV_HEADS, HEAD_DIM)
    vb = v.reshape(B, nb, BLOCK, N_KV_HEADS, HEAD_DIM)
    zpad = jnp.zeros_like(kb[:, :1])
    kk = jnp.concatenate([jnp.concatenate([zpad, kb[:, :-1]], axis=1), kb], axis=2)
    vv = jnp.concatenate([jnp.concatenate([zpad, vb[:, :-1]], axis=1), vb], axis=2)
    scale = HEAD_DIM ** -0.5
    scores = jnp.einsum('bnqhgd,bnkhd->bnhgqk', qb, kk).astype(jnp.float32) * scale
    qpos = jnp.arange(BLOCK)[:, None] + BLOCK
    kpos = jnp.arange(2 * BLOCK)[None, :]
    band = (kpos <= qpos) & (kpos > qpos - WINDOW)
    blk = jnp.arange(nb)[:, None, None]
    valid = band[None] & ((blk > 0) | (kpos[None] >= BLOCK))
    neg = jnp.finfo(jnp.float32).min
    scores = jnp.where(valid[None, :, None, None], scores, neg)
    sink = sinks.astype(jnp.float32).reshape(N_KV_HEADS, Q_PER_KV)[None, None, :, :, None, None]
    m = jnp.maximum(jnp.max(scores, axis=-1, keepdims=True), sink)
    p = jnp.exp(scores - m)
    denom = jnp.sum(p, axis=-1, keepdims=True) + jnp.exp(sink - m)
    probs = (p / denom).astype(vv.dtype)
    out = jnp.einsum('bnhgqk,bnkhd->bnqhgd', probs, vv)
    return out.reshape(B, S, ATTN_WIDTH)


def _chunked_sgu(u, v, vn_g, vn_b, w_s, b_s):
    B, S = v.shape[0], v.shape[1]
    nc = S // CHUNK
    v = _layernorm(v, vn_g, vn_b)
    vc = v.reshape(B, nc, CHUNK, SGU_GROUPS, SGU_GROUP_DIM)
    tril = jnp.tril(jnp.ones((CHUNK, CHUNK), dtype=w_s.dtype))
    w = w_s * tril[None]
    mixed = jnp.einsum('gts,bcsgd->bctgd', w, vc) + b_s.T[None, None, :, :, None]
    return u * mixed.reshape(B, S, SGU_WIDTH)


def setup_inputs(seed: int = 0) -> dict:
    key = jax.random.key(seed)
    ks = jax.random.split(key, 16)
    L, D = DEPTH, D_MODEL
    x = jax.random.normal(ks[0], (BATCH, SEQ, D), jnp.float32)
    ln_in_g = 1.0 + 0.05 * jax.random.normal(ks[1], (D,), jnp.float32)
    ln_in_b = 0.02 * jax.random.normal(ks[2], (D,), jnp.float32)
    col_scale = jnp.concatenate([
        jnp.ones((ATTN_WIDTH + KV_WIDTH,), jnp.float32),
        jnp.full((KV_WIDTH,), BETA, jnp.float32),
        jnp.ones((ATTN_WIDTH,), jnp.float32),
        jnp.full((SGU_WIDTH,), BETA, jnp.float32),
        jnp.ones((2 * SGU_WIDTH + 2 * D,), jnp.float32)])
    w_in = jax.random.normal(ks[3], (L, D, IN_COLS), jnp.float32) * (D ** -0.5) * col_scale
    b_in = 0.02 * jax.random.normal(ks[4], (L, IN_COLS), jnp.float32)
    sinks = 0.5 * jax.random.normal(ks[5], (L, N_Q_HEADS), jnp.float32)
    vn_g = 1.0 + 0.05 * jax.random.normal(ks[6], (L, SGU_WIDTH), jnp.float32)
    vn_b = 0.02 * jax.random.normal(ks[7], (L, SGU_WIDTH), jnp.float32)
    w_s = jax.random.normal(ks[8], (L, SGU_GROUPS, CHUNK, CHUNK), jnp.float32) * (CHUNK ** -0.5)
    b_s = 1.0 + 0.1 * jax.random.normal(ks[9], (L, SGU_GROUPS, CHUNK), jnp.float32)
    p_a = jax.random.normal(ks[10], (L, ATTN_WIDTH, D), jnp.float32) * (ATTN_WIDTH ** -0.5) * BETA
    p_b = jax.random.normal(ks[11], (L, SGU_WIDTH, D), jnp.float32) * (SGU_WIDTH ** -0.5) * BETA
    w_out = jax.random.normal(ks[12], (L, D, D), jnp.float32) * (D ** -0.5) * BETA
    b_out = 0.02 * jax.random.normal(ks[13], (L, D), jnp.float32)
    ln_g = 1.0 + 0.05 * jax.random.normal(ks[14], (L, D), jnp.float32)
    ln_b = 0.02 * jax.random.normal(ks[15], (L, D), jnp.float32)
    return {"x": x, "ln_in_g": ln_in_g, "ln_in_b": ln_in_b, "w_in": w_in, "b_in": b_in,
            "sinks": sinks, "vn_g": vn_g, "vn_b": vn_b, "w_s": w_s, "b_s": b_s,
            "p_a": p_a, "p_b": p_b, "w_out": w_out, "b_out": b_out,
            "ln_g": ln_g, "ln_b": ln_b}


def reference(x, ln_in_g, ln_in_b, w_in, b_in, sinks, vn_g, vn_b, w_s, b_s,
              p_a, p_b, w_out, b_out, ln_g, ln_b):
    x = _layernorm(x, ln_in_g, ln_in_b)
    for l in range(DEPTH):
        h = jnp.einsum('bsd,dc->bsc', x, w_in[l]) + b_in[l]
        q, k, v, g_a, u_b, v_b, g_b, r_a, r_b = _split_cols(h)
        y_a = _swa_sinks(q, k, v, sinks[l]) * jax.nn.silu(g_a)
        y_b = _chunked_sgu(jax.nn.gelu(u_b), jax.nn.gelu(v_b), vn_g[l], vn_b[l],
                           w_s[l], b_s[l]) * jax.nn.silu(g_b)
        merged = (jax.nn.sigmoid(r_a) * jnp.einsum('bsc,cd->bsd', y_a, p_a[l])
                  + jax.nn.sigmoid(r_b) * jnp.einsum('bsc,cd->bsd', y_b, p_b[l]))
        out = jnp.einsum('bsd,de->bse', merged, w_out[l]) + b_out[l]
        x = _layernorm(ALPHA * x + out, ln_g[l], ln_b[l])
    return x
```

```python
import contextlib
import numpy as np
import concourse.bass as bass
import concourse.mybir as mybir
from concourse.bass_utils import run_bass_kernel_spmd

F32 = mybir.dt.float32
BF16 = mybir.dt.bfloat16
AF = mybir.ActivationFunctionType
ALU = mybir.AluOpType

D = 1024
INC = 4864
ALPHA = 4.0 ** 0.25
EPS = 1e-5
R_SLOTS = 4
N_CORES = 8


class _Op:
    __slots__ = ("eng", "fn", "reads", "writes", "chan", "deps", "signal", "ticket", "idx")

    def __init__(self, eng, fn, reads, writes, chan):
        self.eng, self.fn, self.reads, self.writes, self.chan = eng, fn, reads, writes, chan
        self.deps = []
        self.signal = False
        self.ticket = 0


class Prog:
    ENGS = ("pe", "act", "dve", "pool", "sp")

    def __init__(self):
        self.ops = []
        self.last_w = {}
        self.readers = {}
        self.chan_count = {}
        self.chan_batch = {}

    def add(self, eng, fn, reads=(), writes=(), chan=None, batch=False):
        op = _Op(eng, fn, tuple(reads), tuple(writes), chan)
        deps = set()
        for k in op.reads:
            w = self.last_w.get(k)
            if w is not None:
                deps.add(w)
        for k in op.writes:
            w = self.last_w.get(k)
            if w is not None:
                deps.add(w)
            for r in self.readers.get(k, ()):
                deps.add(r)
        deps.discard(op)
        op.deps = list(deps)
        for k in op.reads:
            self.readers.setdefault(k, []).append(op)
        for k in op.writes:
            self.last_w[k] = op
            self.readers[k] = []
        if chan is not None:
            self.chan_count[chan] = self.chan_count.get(chan, 0) + 1
            op.ticket = self.chan_count[chan]
            self.chan_batch[chan] = batch
        op.idx = len(self.ops)
        self.ops.append(op)
        return op

    def emit(self, nc, es):
        for op in self.ops:
            for d in op.deps:
                if d.chan is None:
                    if d.eng == "pe" and op.eng == "pe" and op.chan is None:
                        continue
                    d.signal = True
        cnt = {e: 0 for e in self.ENGS}
        for op in self.ops:
            if op.chan is None and op.signal:
                cnt[op.eng] += 1
                op.ticket = cnt[op.eng]
        esem = {e: es.enter_context(nc.semaphore("e_" + e)) for e in self.ENGS if cnt[e] > 0}
        csem = {c: es.enter_context(nc.semaphore("c_%d" % i)) for i, c in enumerate(self.chan_count)}

        def event(d):
            if d.chan is not None:
                n = self.chan_count[d.chan] if self.chan_batch[d.chan] else d.ticket
                return csem[d.chan], 16 * n
            return esem[d.eng], d.ticket

        per_eng = {e: [] for e in self.ENGS}
        for op in self.ops:
            per_eng[op.eng].append(op)

        def run(ename, e):
            seen = {}
            for op in per_eng[ename]:
                waits = {}
                for d in op.deps:
                    if d.chan is None and d.eng == "pe" and ename == "pe" and op.chan is None:
                        continue
                    s, v = event(d)
                    if v > waits.get(s, (None, 0))[1]:
                        waits[s] = (s, v)
                if op.chan is not None and not self.chan_batch[op.chan] and op.ticket > 1:
                    s = csem[op.chan]
                    v = 16 * (op.ticket - 1)
                    if v > waits.get(s, (None, 0))[1]:
                        waits[s] = (s, v)
                for s, v in waits.values():
                    if seen.get(s, 0) < v:
                        e.wait_ge(s, v)
                        seen[s] = v
                ins = op.fn(e)
                if op.chan is not None:
                    ins.then_inc(csem[op.chan], 16)
                elif op.signal:
                    ins.then_inc(esem[ename], 1)
            if ename == "sp":
                for c, s in csem.items():
                    v = 16 * self.chan_count[c]
                    if seen.get(s, 0) < v:
                        e.wait_ge(s, v)
                        seen[s] = v

        with nc.Block() as block:
            @block.tensor
            def _(e):
                run("pe", e)

            @block.scalar
            def _(e):
                run("act", e)

            @block.vector
            def _(e):
                run("dve", e)

            @block.gpsimd
            def _(e):
                run("pool", e)

            @block.sync
            def _(e):
                run("sp", e)


def build(n_tok, seq_len, layers=(0, 1), do_ln_in=True):
    assert n_tok % 512 == 0 and seq_len % 512 == 0
    NT = n_tok // 512
    nc = bass.Bass("TRN2", target_bir_lowering=False)
    es = contextlib.ExitStack()
    P = Prog()

    def dram(name, shape, dt=F32, kind="ExternalInput"):
        return nc.dram_tensor(name, list(shape), dt, kind=kind).ap()

    x_d = dram("x", [n_tok, D])
    y_d = dram("y", [n_tok, D], kind="ExternalOutput")
    lnin_g = dram("ln_in_g", [1, D])
    lnin_b = dram("ln_in_b", [1, D])
    w_in = dram("w_in", [2, D, INC])
    b_in = dram("b_in", [2, INC])
    sinks = dram("sinks", [1, 16])
    vn_g = dram("vn_g", [2, 512])
    vn_b = dram("vn_b", [2, 512])
    w_s = dram("w_s", [2, 4, 128, 128])
    b_s = dram("b_s", [2, 512])
    p_a = dram("p_a", [2, 512, D])
    p_b = dram("p_b", [2, 512, D])
    w_out = dram("w_out", [2, D, D])
    b_out = dram("b_out", [2, D])
    ln_g = dram("ln_g", [2, D])
    ln_b = dram("ln_b", [2, D])
    ws = [dram("ws%d" % l, [14, 128, 4096], BF16, kind="Internal") for l in range(2)]

    def sb(name, shape, dt=F32):
        return es.enter_context(nc.sbuf_tensor(name, list(shape), dt))

    def psum(name, shape, dt=F32):
        return es.enter_context(nc.psum_tensor(name, list(shape), dt))

    ring = sb("ring", [128, R_SLOTS, 4096], BF16)
    xres = sb("xres", [128, 4, D])
    zbuf = sb("zbuf", [128, 4, D])
    xbf = sb("xbf", [128, 4, D], BF16)
    xT = sb("xT", [128, 8, 512], BF16)
    qT = sb("qT", [128, 4, 512], BF16)
    kpad = [sb("kpad%d" % l, [128, 4, 640], BF16) for l in range(2)]
    vdup = [sb("vdup%d" % l, [128, 5, 256], BF16) for l in range(2)]
    gaT = sb("gaT", [128, 4, 512], BF16)
    uT = sb("uT", [128, 4, 512], BF16)
    gbT = sb("gbT", [128, 4, 512], BF16)
    tRa = sb("tRa", [128, 8, 512], BF16)
    tRb = sb("tRb", [128, 8, 512], BF16)
    vbg = sb("vbg", [128, 2, 512])
    vnh = sb("vnh", [128, 4, 512], BF16)
    PT = sb("PT", [128, 4, 512], BF16)
    rec = sb("rec", [128, 1, 512])
    t1 = sb("t1", [128, 2, 256])
    yaT = sb("yaT", [128, 4, 512], BF16)
    ybT = gbT
    mixed = sb("mixed", [128, 1, 512])
    mT = sb("mT", [128, 8, 512], BF16)
    Ap = sb("Ap", [128, 2, 512], BF16)
    Bp = sb("Bp", [128, 2, 512], BF16)
    stats = sb("stats", [128, 8, 12])
    mv = sb("mv", [128, 8, 2])
    sm = sb("sm", [128, 8, 4])
    maskc = sb("maskc", [128, 2, 512], BF16)
    ident_bf = sb("ident_bf", [128, 128], BF16)
    ident_f = sb("ident_f", [128, 128])
    ones_bf = sb("ones_bf", [128, 128], BF16)
    ones_f = sb("ones_f", [128, 128])
    neghalf = sb("neghalf", [128, 1])
    onesrow = sb("onesrow", [1, 128], BF16)
    twosrow = sb("twosrow", [1, 128], BF16)
    gb = sb("gb", [128, 6, D])
    gT = sb("gT", [128, 6, 8])
    bias_fm = [sb("bias_fm%d" % l, [128, 34]) for l in range(2)]
    halfb = [sb("halfb%d" % l, [128, 16]) for l in range(2)]
    brow = [sb("brow%d" % l, [1, 1792], BF16) for l in range(2)]
    vng = [sb("vng%d" % l, [128, 4]) for l in range(2)]
    wT_bf = [sb("wTbf%d" % l, [128, 4, 128], BF16) for l in range(2)]
    wnat = zbuf[:, 0, 0:512].rearrange("p (g t) -> p g t", g=4)
    wT_f = zbuf[:, 0, 512:1024].rearrange("p (g t) -> p g t", g=4)
    Cc = [sb("Cc%d" % l, [128, 4, 128]) for l in range(2)]
    L2 = zbuf[0:2, 1, 0:512]
    R2 = zbuf[0:2, 1, 512:1024]
    sink_sb = sb("sink_sb", [1, 16])
    sinkexp = sb("sinkexp", [1, 16])
    sinkrow = sb("sinkrow", [1, 4, 512], BF16)

    NB = 7
    ps = [psum("ps%d" % i, [128, 512]) for i in range(NB)]
    psT = psum("psT", [128, 1024], BF16)
    bank_ctr = {}
    POOLS = {"all": list(range(NB)), "A": [0, 1], "B": [2, 3], "C": [4, 5, 6]}

    def next_bank(pool="all"):
        lst = POOLS[pool]
        c = bank_ctr.get(pool, 0)
        bank_ctr[pool] = c + 1
        return lst[c % len(lst)]

    cnt = {}

    def rot(name, n):
        v = cnt.get(name, 0)
        cnt[name] = v + 1
        return v % n

    def mm(out, lhsT, rhs, start, stop, reads, writes):
        P.add("pe", lambda e: e.matmul(out, lhsT, rhs, start=start, stop=stop), reads, writes)

    def act(out, in_, func, reads, writes, bias=None, scale=None):
        kw = {}
        if bias is not None:
            kw["bias"] = bias
        if scale is not None:
            kw["scale"] = scale
        P.add("act", lambda e: e.activation(out=out, in_=in_, func=func, **kw), reads, writes)

    def dma(eng, out, in_, reads, writes, chan, batch=False, slow=False):
        if chan == "const":
            chan = "const_" + eng
        if slow:
            P.add(eng, lambda e: e.dma_start(out=out, in_=in_, allow_slow_non_contiguous=True),
                  reads, writes, chan=chan, batch=batch)
        else:
            P.add(eng, lambda e: e.dma_start(out=out, in_=in_), reads, writes, chan=chan, batch=batch)

    cvt_keys = {}

    def cvt(l, s, dst, src):
        key = ("cvt", l, s, len(cvt_keys.setdefault((l, s), [])))
        cvt_keys[(l, s)].append(key)
        dma("pool", dst, src, (), (key,), chan=("cvt", l, s), batch=True)

    def setup_cvt(l):
        wv = w_in[l].rearrange("(k p) c -> p k c", p=128)

        def slab(s):
            return ws[l][s].rearrange("p (k c) -> p k c", k=8)
        cvt(l, 0, slab(0)[:, :, :], wv[:, :, 0:512])
        cvt(l, 1, slab(1)[:, :, 0:128], wv[:, :, 512:640])
        cvt(l, 1, slab(1)[:, :, 128:192], wv[:, :, 576:640])
        cvt(l, 1, slab(1)[:, :, 192:256], wv[:, :, 512:576])
        cvt(l, 1, slab(1)[:, :, 256:320], wv[:, :, 640:704])
        cvt(l, 1, slab(1)[:, :, 320:384], wv[:, :, 640:704])
        cvt(l, 1, slab(1)[:, :, 384:448], wv[:, :, 704:768])
        cvt(l, 1, slab(1)[:, :, 448:512], wv[:, :, 704:768])
        cvt(l, 2, slab(2)[:, :, :], wv[:, :, 1792:2304])
        cvt(l, 3, slab(3)[:, :, :], wv[:, :, 1280:1792])
        cvt(l, 4, slab(4)[:, :, :], wv[:, :, 768:1280])
        cvt(l, 5, slab(5)[:, :, :], wv[:, :, 2304:2816])
        for i in range(4):
            cvt(l, 6 + i, slab(6 + i)[:, :, :], wv[:, :, 2816 + i * 512: 2816 + (i + 1) * 512])
        cvt(l, 10, ws[l][10].rearrange("p (k c) -> p k c", k=4), p_a[l].rearrange("(k p) c -> p k c", p=128))
        cvt(l, 11, ws[l][11].rearrange("p (k c) -> p k c", k=4), p_b[l].rearrange("(k p) c -> p k c", p=128))
        wo = w_out[l].rearrange("(k p) c -> p k c", p=128)
        for h in range(2):
            cvt(l, 12 + h, slab(12 + h)[:, :, :], wo[:, :, h * 512:(h + 1) * 512])

    def load_x(t, j):
        r0 = t * 512 + j * 128
        dma("sp", xres[:, j, :], x_d[r0:r0 + 128, :], (), (("xres", j),), chan=("xin", j))

    for j in range(4):
        load_x(0, j)

    CH = "const"

    def cdma(out, in_, key, eng="sp", slow=False, reads=(), chan=None):
        if chan is not None:
            dma(eng, out, in_, reads, (key,), chan=chan, batch=False, slow=slow)
        else:
            dma(eng, out, in_, reads, (key,), chan=CH, batch=True, slow=slow)

    P.add("pool", lambda e: e.memset(ones_bf[:], 1.0), (), (("ones_bf",),))
    P.add("pool", lambda e: e.memset(ones_f[:], 1.0), (), (("ones_f",),))
    P.add("pool", lambda e: e.memset(neghalf[:], -0.5), (), (("neghalf",),))
    P.add("pool", lambda e: e.memset(onesrow[:], 1.0), (), (("onesrow",),))
    P.add("pool", lambda e: e.memset(twosrow[:], 2.0), (), (("twosrow",),))
    P.add("pool", lambda e: e.memset(maskc[:], 1.0), (), (("maskc",),))
    P.add("pool", lambda e: e.memset(ident_bf[:], 1.0), (), (("ident_bf",),))
    P.add("pool", lambda e: e.memset(ident_f[:], 1.0), (), (("ident_f",),))
    P.add("pool", lambda e: e.memset(L2[:], 1.0), (), (("z", 1),))
    for l in layers:
        P.add("pool", (lambda l: lambda e: e.memset(kpad[l][:], 0.0))(l), (), (("kpadz", l),))
    P.add("pool", lambda e: e.affine_select(out=maskc[:, 0, :].rearrange("p (g t) -> p g t", g=4),
                                             in_=maskc[:, 0, :].rearrange("p (g t) -> p g t", g=4),
                                             pattern=[[0, 4], [-1, 128]], compare_op=ALU.is_gt, fill=0.0,
                                             base=0, channel_multiplier=1), (("maskc",),), (("maskc",),))
    P.add("pool", lambda e: e.affine_select(out=maskc[:, 1, :].rearrange("p (g t) -> p g t", g=4),
                                             in_=maskc[:, 1, :].rearrange("p (g t) -> p g t", g=4),
                                             pattern=[[0, 4], [1, 128]], compare_op=ALU.is_ge, fill=0.0,
                                             base=0, channel_multiplier=-1), (("maskc",),), (("maskc",),))
    for idt, key in ((ident_bf, "ident_bf"), (ident_f, "ident_f")):
        P.add("pool", (lambda idt: lambda e: e.affine_select(out=idt[:], in_=idt[:], pattern=[[1, 128]],
                                                             compare_op=ALU.is_equal, fill=0.0, base=0,
                                                             channel_multiplier=-1))(idt),
              ((key,),), ((key,),))
    for i, src in enumerate((lnin_g[0], lnin_b[0], ln_g[0], ln_b[0], ln_g[1], ln_b[1])):
        cdma(gb[:, i, :], src.partition_broadcast(128), ("gb", i))
    for i, src in enumerate((lnin_g[0], lnin_b[0], ln_g[0], ln_b[0], ln_g[1], ln_b[1])):
        dma("sp", gT[:, i, :], src.rearrange("(k p) -> p k", p=128), (), (("gT", i),), CH, True, True)
    cdma(sink_sb[:], sinks, ("sink_sb",))
    act(sinkexp[:], sink_sb[:], AF.Exp, (("sink_sb",),), (("sinkexp",),))
    for l in layers:
        bf = bias_fm[l]
        bl = b_in[l]

        def bcols(c0, n):
            return bl[c0:c0 + 128 * n].rearrange("(t p) -> p t", p=128)
        kb = ("bias_fm", l)
        dma("sp", bf[:, 0:4], bcols(0, 4), (), (kb + (0,),), CH, True, True)
        dma("sp", bf[:, 4:5], bcols(512, 1), (), (kb + (1,),), CH, True, True)
        dma("sp", bf[0:64, 5:6], bl[576:640].rearrange("(t p) -> p t", p=64), (), (kb + (2,),), CH, True, True)
        dma("sp", bf[64:128, 5:6], bl[512:576].rearrange("(t p) -> p t", p=64), (), (kb + (3,),), CH, True, True)
        dma("sp", bf[:, 6:10], bcols(1280, 4), (), (kb + (4,),), CH, True, True)
        dma("sp", bf[:, 10:14], bcols(768, 4), (), (kb + (5,),), CH, True, True)
        dma("sp", bf[:, 14:18], bcols(2304, 4), (), (kb + (6,),), CH, True, True)
        dma("sp", bf[:, 18:34], bcols(2816, 16), (), (kb + (7,),), CH, True, True)
        bkeys = tuple(kb + (i,) for i in range(8))
        P.add("dve", (lambda l: lambda e: e.tensor_scalar(out=halfb[l][:], in0=bias_fm[l][:, 18:34], scalar1=0.5,
                                                          scalar2=None, op0=ALU.mult))(l),
              bkeys, (("halfb", l),))
        br = brow[l]
        bl2 = b_in[l:l + 1, :]
        rk = ("brow", l)
        for i, (d0, s0, n) in enumerate(((0, 640, 64), (64, 640, 64), (128, 704, 64), (192, 704, 64), (256, 1792, 512))):
            dma("pool", br[0:1, d0:d0 + n], bl2[:, s0:s0 + n], (), (rk + (i,),), CH, True)
        dma("pool", br[0:1, 768:1792], b_out[l:l + 1, :], (), (rk + (5,),), CH, True)
        dma("sp", vng[l][:], vn_g[l].rearrange("(g p) -> p g", p=128), (), (("vng", l),), CH, True, True)
        for hk in range(2):
            for g in range(4):
                h = hk * 4 + g
                P.add("dve", (lambda l, hk, g, h: lambda e: e.tensor_scalar(
                    out=sinkrow[0:1, l * 2 + hk, g * 128:(g + 1) * 128], in0=ones_f[0:1, :],
                    scalar1=sinkexp[0:1, l * 8 + h:l * 8 + h + 1], scalar2=None, op0=ALU.mult))(l, hk, g, h),
                    (("sinkexp",), ("ones_f",)), (("sinkrow", l, hk, g),))
    for l in layers:
        cdma(wnat[:], w_s[l].rearrange("g t s -> t g s"), ("z", 0), chan="wl")
        bT = next_bank()
        for g in range(4):
            P.add("pe", (lambda g, bT: lambda e: e.transpose(out=ps[bT][:, g * 128:(g + 1) * 128], in_=wnat[:, g, :],
                                                             identity=ident_f[:]))(g, bT),
                  (("z", 0), ("ident_f",)), (("ps", bT),))
        P.add("dve", (lambda bT: lambda e: e.tensor_copy(out=wT_f[:].rearrange("p g t -> p (g t)"), in_=ps[bT][:]))(bT),
              (("ps", bT),), (("z", 0),))
        P.add("pool", lambda e: e.affine_select(out=wT_f[:], in_=wT_f[:], pattern=[[0, 4], [1, 128]],
                                                 compare_op=ALU.is_ge, fill=0.0, base=0, channel_multiplier=-1),
              (("z", 0),), (("z", 0),))
        P.add("dve", (lambda l: lambda e: e.tensor_copy(out=wT_bf[l][:], in_=wT_f[:]))(l), (("z", 0),), (("wT_bf", l),))
        bR = next_bank()
        P.add("pe", (lambda bR: lambda e: e.matmul(ps[bR][0:1, :], ones_f[:, 0:1], wT_f[:].rearrange("p g t -> p (g t)"),
                                                   start=True, stop=True))(bR),
              (("z", 0), ("ones_f",)), (("ps", bR),))
        P.add("dve", (lambda bR: lambda e: e.tensor_copy(out=R2[0:1, :], in_=ps[bR][0:1, :]))(bR),
              (("ps", bR),), (("z", 1),))
        cdma(R2[1:2, :], b_s[l:l + 1, :], ("z", 1), chan="r2l")
        cdma(L2[0:1, :], vn_b[l:l + 1, :], ("z", 1), reads=(("z", 1),), chan="l2l")
        bC = next_bank()
        for g in range(4):
            P.add("pe", (lambda g, bC: lambda e: e.matmul(ps[bC][:, g * 128:(g + 1) * 128], L2[0:2, g * 128:(g + 1) * 128],
                                                          R2[0:2, g * 128:(g + 1) * 128], start=True, stop=True))(g, bC),
                  (("z", 1), ("z", 1), ("z", 1)), (("ps", bC),))
        P.add("dve", (lambda l, bC: lambda e: e.tensor_copy(out=Cc[l][:].rearrange("p g t -> p (g t)"), in_=ps[bC][:]))(l, bC),
              (("ps", bC),), (("Cc", l),))

    for l in layers:
        setup_cvt(l)

    slab_seq = [(t, l, s) for t in range(NT) for l in layers for s in range(14)]
    wstate = {"next_load": 0, "cur": 0}

    def issue_load():
        n = wstate["next_load"]
        if n >= len(slab_seq):
            return
        wstate["next_load"] = n + 1
        _, l, s = slab_seq[n]
        slot = n % R_SLOTS
        dma("sp", ring[:, slot, :], ws[l][s], tuple(cvt_keys[(l, s)]), (("ring", slot),), chan=("ring", slot))

    def acquire(l, s):
        n = wstate["cur"]
        assert slab_seq[n][1] == l and slab_seq[n][2] == s, (slab_seq[n], l, s)
        slot = n % R_SLOTS
        return slot

    def release():
        wstate["cur"] += 1
        issue_load()

    def slab8(slot):
        return ring[:, slot, :].rearrange("p (k c) -> p k c", k=8)

    def slab4(slot):
        return ring[:, slot, :].rearrange("p (k c) -> p k c", k=4)

    XT_ALL = tuple(("xT", j) for j in range(4))

    def ln_p1(src, src_key, eps):
        si = rot("stat", 8)
        st = stats[:, si, :]
        skey = ("stat", si)
        P.add("dve", lambda e: e.bn_stats(out=st[:, 0:6], in_=src[:, 0:512]), (src_key,), (skey,))
        P.add("dve", lambda e: e.bn_stats(out=st[:, 6:12], in_=src[:, 512:1024]), (src_key, skey), (skey,))
        P.add("dve", lambda e: e.bn_aggr(out=mv[:, si, :], in_=st), (skey,), (("mv", si),))
        P.add("dve", lambda e: e.tensor_scalar(out=sm[:, si, 0:1], in0=mv[:, si, 1:2], scalar1=eps, scalar2=None,
                                               op0=ALU.add), (("mv", si),), (("sm0", si),))
        P.add("pool", lambda e: e.tensor_tensor(out=sm[:, si, 1:2], in0=sm[:, si, 0:1], in1=neghalf[:], op=ALU.pow),
              (("sm0", si), ("neghalf",)), (("sm1", si),))
        return si

    deferred_b = []
    deferred_c = []

    def flush_deferred(c=True):
        while deferred_b:
            deferred_b.pop(0)()
        while c and deferred_c:
            deferred_c.pop(0)()

    def ln_p2(si, src, src_key, dst, dst_key, gi, make_bf, after=None):
        P.add("dve", lambda e: e.tensor_scalar(out=sm[:, si, 2:3], in0=mv[:, si, 0:1], scalar1=-1.0,
                                               scalar2=sm[:, si, 1:2], op0=ALU.mult, op1=ALU.mult),
              (("mv", si), ("sm1", si)), (("sm2", si),))
        SK = (("sm1", si), ("sm2", si))
        if make_bf:
            j = make_bf - 1
            bi = rot("xbf", 4)
            act(xbf[:, bi, :], src, AF.Identity, (src_key,) + SK, (("xbf", bi),),
                bias=sm[:, si, 2:3], scale=sm[:, si, 1:2])

            def part_b():
                for kc in range(8):
                    P.add("pe", (lambda kc, bi: lambda e: e.transpose(out=psT[:, kc * 128:(kc + 1) * 128],
                                                                     in_=xbf[:, bi, kc * 128:(kc + 1) * 128],
                                                                     identity=ident_bf[:]))(kc, bi),
                          (("xbf", bi), ("ident_bf",)), (("psT",),))
                xo = xT[:, :, j * 128:(j + 1) * 128]
                P.add("dve", lambda e: e.tensor_tensor(out=xo, in0=psT[:].rearrange("p (k t) -> p k t", k=8),
                                                       in1=gT[:, gi, :].unsqueeze(2).to_broadcast([128, 8, 128]),
                                                       op=ALU.mult),
                      (("psT",), ("gT", gi)), (("xT", j),))
                P.add("dve", lambda e: e.tensor_tensor(out=xo, in0=xo,
                                                       in1=gT[:, gi + 1, :].unsqueeze(2).to_broadcast([128, 8, 128]),
                                                       op=ALU.add),
                      (("xT", j), ("gT", gi + 1)), (("xT", j),))
            deferred_b.append(part_b)
        act(dst, src, AF.Identity, (src_key,) + SK, (dst_key,), bias=sm[:, si, 2:3], scale=sm[:, si, 1:2])

        def part_c():
            P.add("pool", lambda e: e.tensor_tensor(out=dst, in0=dst, in1=gb[:, gi, :], op=ALU.mult),
                  (dst_key, ("gb", gi)), (dst_key,))
            P.add("pool", lambda e: e.tensor_tensor(out=dst, in0=dst, in1=gb[:, gi + 1, :], op=ALU.add),
                  (dst_key, ("gb", gi + 1)), (dst_key,))
            if after is not None:
                after()
        deferred_c.append(part_c)

    def to_xT(src, src_key, j):
        bi = rot("xbf", 4)
        act(xbf[:, bi, :], src, AF.Copy, (src_key,), (("xbf", bi),))
        for kc in range(8):
            P.add("pe", (lambda kc, bi: lambda e: e.transpose(out=psT[:, kc * 128:(kc + 1) * 128],
                                                             in_=xbf[:, bi, kc * 128:(kc + 1) * 128],
                                                             identity=ident_bf[:]))(kc, bi),
                  (("xbf", bi), ("ident_bf",)), (("psT",),))
        act(xT[:, :, j * 128:(j + 1) * 128], psT[:].rearrange("p (k t) -> p k t", k=8), AF.Copy,
            (("psT",),), (("xT", j),))

    lnin_si = {}

    def lnin_step(j):
        if j < 4:
            lnin_si[j] = ln_p1(xres[:, j, :], ("xres", j), EPS)
        if j >= 1:
            ln_p2(lnin_si[j - 1], xres[:, j - 1, :], ("xres", j - 1), xres[:, j - 1, :], ("xres", j - 1), 0, j)

    def layer(l, t, last):
        first_tile_of_seq = (t * 512) % seq_len == 0
        bfm = bias_fm[l]
        BK = tuple(("bias_fm", l, i) for i in range(8))

        def fm_tile(slot, col, out_fn, reads_extra=()):
            b = next_bank("A")
            sl = slab8(slot)
            for kc in range(8):
                mm(ps[b][:], sl[:, kc, col * 128:(col + 1) * 128], xT[:, kc, :], kc == 0, kc == 7,
                   (("ring", slot),) + XT_ALL, (("ps", b),))
            out_fn(b)

        slot = acquire(l, 0)
        for c in range(4):
            fm_tile(slot, c, (lambda c: lambda b: act(qT[:, c, :], ps[b][:], AF.Identity, (("ps", b),) + BK,
                                                      (("qT", c),), bias=bfm[:, c:c + 1]))(c))
        release()
        slot = acquire(l, 1)
        kp = kpad[l]

        def k_out(b, col, var_lo, var_hi):
            act(kp[0:64, var_lo, 128:640], ps[b][0:64, :], AF.Identity, (("ps", b), ("kpadz", l)) + BK,
                (("kpad", l, var_lo),), bias=bfm[0:64, col:col + 1])
            act(kp[64:128, var_hi, 128:640], ps[b][64:128, :], AF.Identity, (("ps", b), ("kpadz", l)) + BK,
                (("kpad", l, var_hi),), bias=bfm[64:128, col:col + 1])
        fm_tile(slot, 0, lambda b: k_out(b, 4, 0, 3))
        fm_tile(slot, 1, lambda b: k_out(b, 5, 2, 1))
        sl = slab8(slot)
        RK = tuple(("brow", l, i) for i in range(6))
        for j in range(4):
            b = next_bank()
            for kc in range(8):
                mm(ps[b][:, 0:256], xT[:, kc, j * 128:(j + 1) * 128], sl[:, kc, 256:512], kc == 0, False,
                   (("ring", slot), ("xT", j)), (("ps", b),))
            mm(ps[b][:, 0:256], onesrow[0:1, :], brow[l][0:1, 0:256], False, True, RK + (("onesrow",),), (("ps", b),))
            act(vdup[l][:, 1 + j, :], ps[b][:, 0:256], AF.Copy, (("ps", b),), (("vdup", l, 1 + j),))
        release()
        slot = acquire(l, 2)
        sl = slab8(slot)
        for j in range(4):
            b = next_bank()
            for kc in range(8):
                mm(ps[b][:], xT[:, kc, j * 128:(j + 1) * 128], sl[:, kc, :], kc == 0, False,
                   (("ring", slot), ("xT", j)), (("ps", b),))
            mm(ps[b][:], onesrow[0:1, :], brow[l][0:1, 256:768], False, True, RK + (("onesrow",),), (("ps", b),))
            vi = rot("vbg", 2)
            act(vbg[:, vi, :], ps[b][:], AF.Gelu_apprx_tanh, (("ps", b),), (("vbg", vi),))
            si = rot("stat", 8)
            P.add("dve", (lambda vi, si: lambda e: e.bn_stats(out=stats[:, si, 0:6], in_=vbg[:, vi, :]))(vi, si),
                  (("vbg", vi),), (("stat", si),))
            P.add("dve", (lambda si: lambda e: e.bn_aggr(out=mv[:, si, :], in_=stats[:, si, 0:6]))(si),
                  (("stat", si),), (("mv", si),))
            P.add("dve", (lambda si: lambda e: e.tensor_scalar(out=sm[:, si, 0:1], in0=mv[:, si, 1:2], scalar1=EPS,
                                                               scalar2=None, op0=ALU.add))(si),
                  (("mv", si),), (("sm0", si),))
            P.add("pool", (lambda si: lambda e: e.tensor_tensor(out=sm[:, si, 1:2], in0=sm[:, si, 0:1], in1=neghalf[:],
                                                                op=ALU.pow))(si),
                  (("sm0", si), ("neghalf",)), (("sm1", si),))
            P.add("dve", (lambda vi, si, j: lambda e: e.tensor_scalar(out=vnh[:, j, :], in0=vbg[:, vi, :],
                                                                      scalar1=mv[:, si, 0:1], scalar2=sm[:, si, 1:2],
                                                                      op0=ALU.subtract, op1=ALU.mult))(vi, si, j),
                  (("vbg", vi), ("mv", si), ("sm1", si)), (("vnh", j),))
        release()
        slot = acquire(l, 3)
        for c in range(4):
            fm_tile(slot, c, (lambda c: lambda b: act(uT[:, c, :], ps[b][:], AF.Gelu_apprx_tanh, (("ps", b),) + BK,
                                                      (("uT", c),), bias=bfm[:, 6 + c:7 + c]))(c))
        release()
        slot = acquire(l, 4)
        for c in range(4):
            fm_tile(slot, c, (lambda c: lambda b: act(gaT[:, c, :], ps[b][:], AF.Silu, (("ps", b),) + BK,
                                                      (("gaT", c),), bias=bfm[:, 10 + c:11 + c]))(c))
        release()
        slot = acquire(l, 5)
        for c in range(4):
            fm_tile(slot, c, (lambda c: lambda b: act(gbT[:, c, :], ps[b][:], AF.Silu, (("ps", b),) + BK,
                                                      (("gbT", c),) + tuple(("ybT", jj) for jj in range(4)),
                                                      bias=bfm[:, 14 + c:15 + c]))(c))
        release()
        for c in range(4):
            P.add("pool", (lambda c: lambda e: e.tensor_tensor(out=uT[:, c, :], in0=uT[:, c, :], in1=gbT[:, c, :],
                                                               op=ALU.mult))(c),
                  (("uT", c), ("gbT", c)), (("uT", c),))

        def attn_scores(j, hk):
            first = first_tile_of_seq and j == 0
            kbs = [1] if first else [0, 1]
            pts = []
            for kbi in kbs:
                cb = j + kbi
                bS = next_bank("B")
                for g in range(4):
                    var = hk * 2 + (g % 2)
                    mm(ps[bS][:, g * 128:(g + 1) * 128], kp[:, var, cb * 128:(cb + 1) * 128],
                       qT[:, 2 * hk + g // 2, j * 128:(j + 1) * 128], True, True,
                       (("kpad", l, var), ("qT", 2 * hk + g // 2), ("kpadz", l)), (("ps", bS),))
                pi = rot("PT", 4)
                act(PT[:, pi, :], ps[bS][:], AF.Exp, (("ps", bS),), (("PT", pi),), scale=0.125)
                P.add("dve", (lambda pi, kbi: lambda e: e.tensor_tensor(out=PT[:, pi, :], in0=PT[:, pi, :],
                                                                       in1=maskc[:, kbi, :], op=ALU.mult))(pi, kbi),
                      (("PT", pi), ("maskc",)), (("PT", pi),))
                pts.append((pi, cb))
            return pts

        def attn_pv(j, hk, pts):
            def gsel(ap, b):
                return ap.rearrange("p (a b t) -> p a b t", a=2, b=2)[:, :, b, :]
            bO = next_bank("C")
            for b in range(2):
                for i, (pi, cb) in enumerate(pts):
                    mm(ps[bO][b * 64:(b + 1) * 64, 0:256].rearrange("p (a t) -> p a t", a=2),
                       vdup[l][:, cb, hk * 128:hk * 128 + 64], gsel(PT[:, pi, :], b), i == 0, i == len(pts) - 1,
                       (("vdup", l, cb), ("PT", pi)), (("ps", bO),))
            bD = next_bank("C")
            for b in range(2):
                for i, (pi, cb) in enumerate(pts):
                    mm(ps[bD][b * 64:(b + 1) * 64, 0:256].rearrange("p (a t) -> p a t", a=2),
                       ones_bf[:, 0:64], gsel(PT[:, pi, :], b), i == 0, False,
                       (("ones_bf",), ("PT", pi)), (("ps", bD),))
                mm(ps[bD][b * 64:(b + 1) * 64, 0:256].rearrange("p (a t) -> p a t", a=2),
                   onesrow[0:1, 0:64], gsel(sinkrow[0:1, l * 2 + hk, :], b), False, True,
                   tuple(("sinkrow", l, hk, g) for g in range(4)) + (("onesrow",),), (("ps", bD),))
            ri = rot("rec", 1)
            P.add("dve", (lambda ri, bD: lambda e: e.reciprocal(out=rec[:, ri, 0:256], in_=ps[bD][:, 0:256]))(ri, bD),
                  (("ps", bD),), (("rec", ri),))
            ti = rot("t1", 2)
            P.add("dve", (lambda ti, ri, bO: lambda e: e.tensor_tensor(
                out=t1[:, ti, :], in0=ps[bO][:, 0:256], in1=rec[:, ri, 0:256], op=ALU.mult))(ti, ri, bO),
                (("ps", bO), ("rec", ri)), (("t1", ti),))
            P.add("dve", (lambda ti, hk, j: lambda e: e.tensor_tensor(
                out=yaT[:, 2 * hk:2 * hk + 2, j * 128:(j + 1) * 128],
                in0=t1[:, ti, :].rearrange("p (a t) -> p a t", a=2),
                in1=gaT[:, 2 * hk:2 * hk + 2, j * 128:(j + 1) * 128], op=ALU.mult))(ti, hk, j),
                (("t1", ti), ("gaT", 2 * hk), ("gaT", 2 * hk + 1)), (("yaT", j, hk),))

        def sgu_block(j):
            bM = next_bank("C")
            for g in range(4):
                mm(ps[bM][:, g * 128:(g + 1) * 128], vnh[:, j, g * 128:(g + 1) * 128], wT_bf[l][:, g, :], True, True,
                   (("vnh", j), ("wT_bf", l)), (("ps", bM),))
            mi = rot("mixed", 1)
            for g in range(4):
                P.add("dve", (lambda g, mi, bM: lambda e: e.scalar_tensor_tensor(
                    out=mixed[:, mi, g * 128:(g + 1) * 128], in0=ps[bM][:, g * 128:(g + 1) * 128],
                    scalar=vng[l][:, g:g + 1], in1=Cc[l][:, g, :], op0=ALU.mult, op1=ALU.add))(g, mi, bM),
                    (("ps", bM), ("vng", l), ("Cc", l), ("mixed", mi)), (("mixed", mi),))
            P.add("dve", (lambda mi, j: lambda e: e.tensor_tensor(
                out=ybT[:, :, j * 128:(j + 1) * 128], in0=mixed[:, mi, :].rearrange("p (g t) -> p g t", g=4),
                in1=uT[:, :, j * 128:(j + 1) * 128], op=ALU.mult))(mi, j),
                (("mixed", mi),) + tuple(("uT", c) for c in range(4)), (("ybT", j),) + tuple(("gbT", c) for c in range(4)))

        units = [(j, hk) for j in range(4) for hk in range(2)]
        pend = {}
        ti_ = 0
        for i in range(4):
            slot = acquire(l, 6 + i)
            dst = tRa if i < 2 else tRb
            for c in range(4):
                m = (i % 2) * 4 + c
                col = (i * 4 + c)
                fm_tile(slot, c, (lambda dst, m, col: lambda b: act(dst[:, m, :], ps[b][:], AF.Tanh,
                                                                    (("ps", b), ("halfb", l)),
                                                                    ((dst is tRa and "tRa" or "tRb", m),),
                                                                    bias=halfb[l][:, col:col + 1], scale=0.5))(dst, m, col))
                if ti_ % 2 == 0:
                    u = ti_ // 2
                    if u - 1 in pend:
                        attn_pv(units[u - 1][0], units[u - 1][1], pend.pop(u - 1))
                    pend[u] = attn_scores(*units[u])
                elif ti_ // 2 < 4:
                    sgu_block(ti_ // 2)
                ti_ += 1
            release()
        attn_pv(units[7][0], units[7][1], pend.pop(7))
        P.add("pool", lambda e: e.tensor_copy(out=kp[:, :, 0:128], in_=kp[:, :, 512:640]),
              tuple(("kpad", l, v) for v in range(4)), tuple(("kpad", l, v) for v in range(4)))
        P.add("pool", lambda e: e.tensor_copy(out=vdup[l][:, 0, :], in_=vdup[l][:, 4, :]),
              (("vdup", l, 4),), (("vdup", l, 0),))

        sa = acquire(l, 10)
        sA = slab4(sa)
        sbn = (wstate["cur"] + 1) % R_SLOTS
        YA = tuple(("yaT", j, hk) for j in range(4) for hk in range(2))
        YB = tuple(("ybT", j) for j in range(4))
        sB = slab4(sbn)
        for m in range(8):
            bA = next_bank()
            for kc in range(4):
                mm(ps[bA][:], sA[:, kc, m * 128:(m + 1) * 128], yaT[:, kc, :], kc == 0, kc == 3,
                   (("ring", sa),) + YA, (("ps", bA),))
            bB = next_bank()
            for kc in range(4):
                mm(ps[bB][:], sB[:, kc, m * 128:(m + 1) * 128], ybT[:, kc, :], kc == 0, kc == 3,
                   (("ring", sbn),) + YB, (("ps", bB),))
            ai = rot("Ap", 2)
            P.add("dve", (lambda ai, bA, m: lambda e: e.scalar_tensor_tensor(
                out=Ap[:, ai, :], in0=tRa[:, m, :], scalar=1.0, in1=ps[bA][:], op0=ALU.add, op1=ALU.mult))(ai, bA, m),
                (("tRa", m), ("ps", bA)), (("Ap", ai),))
            P.add("dve", (lambda ai, bB, m: lambda e: e.scalar_tensor_tensor(
                out=Bp[:, ai, :], in0=tRb[:, m, :], scalar=1.0, in1=ps[bB][:], op0=ALU.add, op1=ALU.mult))(ai, bB, m),
                (("tRb", m), ("ps", bB)), (("Bp", ai),))
            P.add("pool", (lambda ai, m: lambda e: e.tensor_tensor(out=mT[:, m, :], in0=Ap[:, ai, :], in1=Bp[:, ai, :],
                                                                   op=ALU.add))(ai, m),
                  (("Ap", ai), ("Bp", ai)), (("mT", m),))
        release()
        s2 = acquire(l, 11)
        assert s2 == sbn
        release()
        def ln_block2(j, zi, si):
            if last:
                r0 = t * 512 + j * 128

                def out_dma():
                    dma("sp", y_d[r0:r0 + 128, :], zbuf[:, zi, :], (("z", zi),), (("y", t, j),), chan=("out", zi))
                ln_p2(si, zbuf[:, zi, :], ("z", zi), zbuf[:, zi, :], ("z", zi), 2 + 2 * l, 0, after=out_dma)
            else:
                ln_p2(si, zbuf[:, zi, :], ("z", zi), xres[:, j, :], ("xres", j), 2 + 2 * l, j + 1)

        so0 = acquire(l, 12)
        so1 = (wstate["cur"] + 1) % R_SLOTS
        MT = tuple(("mT", m) for m in range(8))
        zis = []
        sis = []
        for j in range(4):
            zi = rot("z", 4)
            zis.append(zi)
            for h, so in ((0, so0), (1, so1)):
                b = next_bank()
                sl = slab8(so)
                for kc in range(8):
                    mm(ps[b][:], mT[:, kc, j * 128:(j + 1) * 128], sl[:, kc, :], kc == 0, False,
                       (("ring", so),) + MT, (("ps", b),))
                mm(ps[b][:], twosrow[0:1, :], brow[l][0:1, 768 + h * 512:768 + (h + 1) * 512], False, True,
                   RK + (("twosrow",),), (("ps", b),))
                P.add("dve", (lambda zi, h, b, j: lambda e: e.scalar_tensor_tensor(
                    out=zbuf[:, zi, h * 512:(h + 1) * 512], in0=xres[:, j, h * 512:(h + 1) * 512], scalar=2.0 * ALPHA,
                    in1=ps[b][:], op0=ALU.mult, op1=ALU.add))(zi, h, b, j),
                    (("xres", j), ("ps", b), ("z", zi)), (("z", zi),))
            nxt = last and t + 1 < NT
            if nxt:
                load_x(t + 1, j)
            nxt_ln = nxt and do_ln_in
            if nxt_ln and j >= 1:
                lnin_si[j - 1] = ln_p1(xres[:, j - 1, :], ("xres", j - 1), EPS)
            sis.append(ln_p1(zbuf[:, zi, :], ("z", zi), 4.0 * EPS))
            if j >= 1:
                ln_block2(j - 1, zis[j - 1], sis[j - 1])
            if not last and j >= 2:
                deferred_b.pop(0)()
            if nxt_ln and j >= 1:
                ln_p2(lnin_si[j - 1], xres[:, j - 1, :], ("xres", j - 1), xres[:, j - 1, :], ("xres", j - 1), 0, j)
        if last and t + 1 < NT and do_ln_in:
            lnin_si[3] = ln_p1(xres[:, 3, :], ("xres", 3), EPS)
        ln_block2(3, zis[3], sis[3])
        if last and t + 1 < NT and do_ln_in:
            ln_p2(lnin_si[3], xres[:, 3, :], ("xres", 3), xres[:, 3, :], ("xres", 3), 0, 4)
        elif last and t + 1 < NT:
            for jj in range(4):
                to_xT(xres[:, jj, :], ("xres", jj), jj)
        flush_deferred()
        release()
        s3 = acquire(l, 13)
        assert s3 == so1
        release()

    for _ in range(R_SLOTS):
        issue_load()
    for t in range(NT):
        if t == 0:
            if do_ln_in:
                for jj in range(5):
                    lnin_step(jj)
            else:
                for jj in range(4):
                    to_xT(xres[:, jj, :], ("xres", jj), jj)
            flush_deferred()
        for li, l in enumerate(layers):
            layer(l, t, li == len(layers) - 1)

    flush_deferred()
    P.emit(nc, es)
    es.close()
    return nc


_CACHE = {}


def _get_nc(n_tok, seq_len, layers, do_ln_in):
    key = (n_tok, seq_len, tuple(layers), do_ln_in)
    if key not in _CACHE:
        _CACHE[key] = build(n_tok, seq_len, layers, do_ln_in)
    return _CACHE[key]


def _run(x, params, layers=(0, 1), do_ln_in=True, n_cores=N_CORES):
    B, S, _ = x.shape
    per = B // n_cores
    n_tok = per * S
    nc = _get_nc(n_tok, S, layers, do_ln_in)
    f = lambda a: np.ascontiguousarray(np.asarray(a, dtype=np.float32))
    shared = {
        "ln_in_g": f(params["ln_in_g"]).reshape(1, D), "ln_in_b": f(params["ln_in_b"]).reshape(1, D),
        "w_in": f(params["w_in"]), "b_in": f(params["b_in"]), "sinks": f(params["sinks"]).reshape(1, 16),
        "vn_g": f(params["vn_g"]), "vn_b": f(params["vn_b"]), "w_s": f(params["w_s"]),
        "b_s": f(params["b_s"]).reshape(2, 512), "p_a": f(params["p_a"]), "p_b": f(params["p_b"]),
        "w_out": f(params["w_out"]), "b_out": f(params["b_out"]), "ln_g": f(params["ln_g"]), "ln_b": f(params["ln_b"]),
    }
    xf = f(x).reshape(n_cores, n_tok, D)
    in_maps = [dict(shared, x=xf[c]) for c in range(n_cores)]
    res = run_bass_kernel_spmd(nc, in_maps, core_ids=list(range(n_cores)))
    out = np.stack([np.asarray(r["y"]) for r in res.results], axis=0)
    return out.reshape(B, S, D).astype(np.float32)


def kernel(x, ln_in_g, ln_in_b, w_in, b_in, sinks, vn_g, vn_b, w_s, b_s, p_a, p_b, w_out, b_out, ln_g, ln_b):
    params = dict(ln_in_g=ln_in_g, ln_in_b=ln_in_b, w_in=w_in, b_in=b_in, sinks=sinks, vn_g=vn_g, vn_b=vn_b,
                  w_s=w_s, b_s=b_s, p_a=p_a, p_b=p_b, w_out=w_out, b_out=b_out, ln_g=ln_g, ln_b=ln_b)
    return _run(np.asarray(x), params)
```

```python
import contextlib
import numpy as np
import concourse.bass as bass
import concourse.mybir as mybir
from concourse.bass_utils import run_bass_kernel_spmd

F32 = mybir.dt.float32
BF16 = mybir.dt.bfloat16
AF = mybir.ActivationFunctionType
ALU = mybir.AluOpType

D = 1024
INC = 4864
ALPHA = 4.0 ** 0.25
EPS = 1e-5
R_SLOTS = 4
N_CORES = 8


class _Op:
    __slots__ = ("eng", "fn", "reads", "writes", "chan", "deps", "signal", "ticket", "idx")

    def __init__(self, eng, fn, reads, writes, chan):
        self.eng, self.fn, self.reads, self.writes, self.chan = eng, fn, reads, writes, chan
        self.deps = []
        self.signal = False
        self.ticket = 0


class Prog:
    ENGS = ("pe", "act", "dve", "pool", "sp")

    def __init__(self):
        self.ops = []
        self.last_w = {}
        self.readers = {}
        self.chan_count = {}
        self.chan_batch = {}

    def add(self, eng, fn, reads=(), writes=(), chan=None, batch=False):
        op = _Op(eng, fn, tuple(reads), tuple(writes), chan)
        deps = set()
        for k in op.reads:
            w = self.last_w.get(k)
            if w is not None:
                deps.add(w)
        for k in op.writes:
            w = self.last_w.get(k)
            if w is not None:
                deps.add(w)
            for r in self.readers.get(k, ()):
                deps.add(r)
        deps.discard(op)
        op.deps = list(deps)
        for k in op.reads:
            self.readers.setdefault(k, []).append(op)
        for k in op.writes:
            self.last_w[k] = op
            self.readers[k] = []
        if chan is not None:
            self.chan_count[chan] = self.chan_count.get(chan, 0) + 1
            op.ticket = self.chan_count[chan]
            self.chan_batch[chan] = batch
        op.idx = len(self.ops)
        self.ops.append(op)
        return op

    def emit(self, nc, es):
        for op in self.ops:
            for d in op.deps:
                if d.chan is None:
                    if d.eng == "pe" and op.eng == "pe" and op.chan is None:
                        continue
                    d.signal = True
        cnt = {e: 0 for e in self.ENGS}
        for op in self.ops:
            if op.chan is None and op.signal:
                cnt[op.eng] += 1
                op.ticket = cnt[op.eng]
        esem = {e: es.enter_context(nc.semaphore("e_" + e)) for e in self.ENGS if cnt[e] > 0}
        csem = {c: es.enter_context(nc.semaphore("c_%d" % i)) for i, c in enumerate(self.chan_count)}

        def event(d):
            if d.chan is not None:
                n = self.chan_count[d.chan] if self.chan_batch[d.chan] else d.ticket
                return csem[d.chan], 16 * n
            return esem[d.eng], d.ticket

        per_eng = {e: [] for e in self.ENGS}
        for op in self.ops:
            per_eng[op.eng].append(op)

        def run(ename, e):
            seen = {}
            for op in per_eng[ename]:
                waits = {}
                for d in op.deps:
                    if d.chan is None and d.eng == "pe" and ename == "pe" and op.chan is None:
                        continue
                    s, v = event(d)
                    if v > waits.get(s, (None, 0))[1]:
                        waits[s] = (s, v)
                if op.chan is not None and not self.chan_batch[op.chan] and op.ticket > 1:
                    s = csem[op.chan]
                    v = 16 * (op.ticket - 1)
                    if v > waits.get(s, (None, 0))[1]:
                        waits[s] = (s, v)
                for s, v in waits.values():
                    if seen.get(s, 0) < v:
                        e.wait_ge(s, v)
                        seen[s] = v
                ins = op.fn(e)
                if op.chan is not None:
                    ins.then_inc(csem[op.chan], 16)
                elif op.signal:
                    ins.then_inc(esem[ename], 1)
            if ename == "sp":
                for c, s in csem.items():
                    v = 16 * self.chan_count[c]
                    if seen.get(s, 0) < v:
                        e.wait_ge(s, v)
                        seen[s] = v

        with nc.Block() as block:
            @block.tensor
            def _(e):
                run("pe", e)

            @block.scalar
            def _(e):
                run("act", e)

            @block.vector
            def _(e):
                run("dve", e)

            @block.gpsimd
            def _(e):
                run("pool", e)

            @block.sync
            def _(e):
                run("sp", e)


def build(n_tok, seq_len, layers=(0, 1), do_ln_in=True):
    assert n_tok % 512 == 0 and seq_len % 512 == 0
    NT = n_tok // 512
    nc = bass.Bass("TRN2", target_bir_lowering=False)
    es = contextlib.ExitStack()
    P = Prog()

    def dram(name, shape, dt=F32, kind="ExternalInput"):
        return nc.dram_tensor(name, list(shape), dt, kind=kind).ap()

    x_d = dram("x", [n_tok, D])
    y_d = dram("y", [n_tok, D], kind="ExternalOutput")
    lnin_g = dram("ln_in_g", [1, D])
    lnin_b = dram("ln_in_b", [1, D])
    w_in = dram("w_in", [2, D, INC])
    b_in = dram("b_in", [2, INC])
    sinks = dram("sinks", [1, 16])
    vn_g = dram("vn_g", [2, 512])
    vn_b = dram("vn_b", [2, 512])
    w_s = dram("w_s", [2, 4, 128, 128])
    b_s = dram("b_s", [2, 512])
    p_a = dram("p_a", [2, 512, D])
    p_b = dram("p_b", [2, 512, D])
    w_out = dram("w_out", [2, D, D])
    b_out = dram("b_out", [2, D])
    ln_g = dram("ln_g", [2, D])
    ln_b = dram("ln_b", [2, D])
    ws = [dram("ws%d" % l, [14, 128, 4096], BF16, kind="Internal") for l in range(2)]

    def sb(name, shape, dt=F32):
        return es.enter_context(nc.sbuf_tensor(name, list(shape), dt))

    def psum(name, shape, dt=F32):
        return es.enter_context(nc.psum_tensor(name, list(shape), dt))

    ring = sb("ring", [128, R_SLOTS, 4096], BF16)
    xres = sb("xres", [128, 4, D])
    zbuf = sb("zbuf", [128, 4, D])
    xbf = sb("xbf", [128, 4, D], BF16)
    xT = sb("xT", [128, 8, 512], BF16)
    qT = sb("qT", [128, 4, 512], BF16)
    kpad = [sb("kpad%d" % l, [128, 4, 640], BF16) for l in range(2)]
    vdup = [sb("vdup%d" % l, [128, 5, 256], BF16) for l in range(2)]
    gaT = sb("gaT", [128, 4, 512], BF16)
    uT = sb("uT", [128, 4, 512], BF16)
    gbT = sb("gbT", [128, 4, 512], BF16)
    tRa = sb("tRa", [128, 8, 512], BF16)
    tRb = sb("tRb", [128, 8, 512], BF16)
    vbg = sb("vbg", [128, 2, 512])
    vnh = sb("vnh", [128, 4, 512], BF16)
    PT = sb("PT", [128, 4, 512], BF16)
    rec = sb("rec", [128, 1, 512])
    t1 = sb("t1", [128, 2, 256])
    yaT = sb("yaT", [128, 4, 512], BF16)
    ybT = gbT
    mixed = sb("mixed", [128, 1, 512])
    mT = sb("mT", [128, 8, 512], BF16)
    Ap = sb("Ap", [128, 2, 512], BF16)
    Bp = sb("Bp", [128, 2, 512], BF16)
    stats = sb("stats", [128, 8, 12])
    mv = sb("mv", [128, 8, 2])
    sm = sb("sm", [128, 8, 4])
    maskc = sb("maskc", [128, 2, 512], BF16)
    ident_bf = sb("ident_bf", [128, 128], BF16)
    ident_f = sb("ident_f", [128, 128])
    ones_bf = sb("ones_bf", [128, 128], BF16)
    ones_f = sb("ones_f", [128, 128])
    neghalf = sb("neghalf", [128, 1])
    onesrow = sb("onesrow", [1, 128], BF16)
    twosrow = sb("twosrow", [1, 128], BF16)
    gb = sb("gb", [128, 6, D])
    gT = sb("gT", [128, 6, 8])
    bias_fm = [sb("bias_fm%d" % l, [128, 34]) for l in range(2)]
    halfb = [sb("halfb%d" % l, [128, 16]) for l in range(2)]
    brow = [sb("brow%d" % l, [1, 1792], BF16) for l in range(2)]
    vng = [sb("vng%d" % l, [128, 4]) for l in range(2)]
    wT_bf = [sb("wTbf%d" % l, [128, 4, 128], BF16) for l in range(2)]
    wnat = zbuf[:, 0, 0:512].rearrange("p (g t) -> p g t", g=4)
    wT_f = zbuf[:, 0, 512:1024].rearrange("p (g t) -> p g t", g=4)
    Cc = [sb("Cc%d" % l, [128, 4, 128]) for l in range(2)]
    L2 = zbuf[0:2, 1, 0:512]
    R2 = zbuf[0:2, 1, 512:1024]
    sink_sb = sb("sink_sb", [1, 16])
    sinkexp = sb("sinkexp", [1, 16])
    sinkrow = sb("sinkrow", [1, 4, 512], BF16)

    NB = 7
    ps = [psum("ps%d" % i, [128, 512]) for i in range(NB)]
    psT = psum("psT", [128, 1024], BF16)
    bank_ctr = {}
    POOLS = {"all": list(range(NB)), "A": [0, 1], "B": [2, 3], "C": [4, 5, 6]}

    def next_bank(pool="all"):
        lst = POOLS[pool]
        c = bank_ctr.get(pool, 0)
        bank_ctr[pool] = c + 1
        return lst[c % len(lst)]

    cnt = {}

    def rot(name, n):
        v = cnt.get(name, 0)
        cnt[name] = v + 1
        return v % n

    def mm(out, lhsT, rhs, start, stop, reads, writes):
        P.add("pe", lambda e: e.matmul(out, lhsT, rhs, start=start, stop=stop), reads, writes)

    def act(out, in_, func, reads, writes, bias=None, scale=None):
        kw = {}
        if bias is not None:
            kw["bias"] = bias
        if scale is not None:
            kw["scale"] = scale
        P.add("act", lambda e: e.activation(out=out, in_=in_, func=func, **kw), reads, writes)

    def dma(eng, out, in_, reads, writes, chan, batch=False, slow=False):
        if chan == "const":
            chan = "const_" + eng
        if slow:
            P.add(eng, lambda e: e.dma_start(out=out, in_=in_, allow_slow_non_contiguous=True),
                  reads, writes, chan=chan, batch=batch)
        else:
            P.add(eng, lambda e: e.dma_start(out=out, in_=in_), reads, writes, chan=chan, batch=batch)

    cvt_keys = {}

    def cvt(l, s, dst, src):
        key = ("cvt", l, s, len(cvt_keys.setdefault((l, s), [])))
        cvt_keys[(l, s)].append(key)
        dma("pool", dst, src, (), (key,), chan=("cvt", l, s), batch=True)

    def setup_cvt(l):
        wv = w_in[l].rearrange("(k p) c -> p k c", p=128)

        def slab(s):
            return ws[l][s].rearrange("p (k c) -> p k c", k=8)
        cvt(l, 0, slab(0)[:, :, :], wv[:, :, 0:512])
        cvt(l, 1, slab(1)[:, :, 0:128], wv[:, :, 512:640])
        cvt(l, 1, slab(1)[:, :, 128:192], wv[:, :, 576:640])
        cvt(l, 1, slab(1)[:, :, 192:256], wv[:, :, 512:576])
        cvt(l, 1, slab(1)[:, :, 256:320], wv[:, :, 640:704])
        cvt(l, 1, slab(1)[:, :, 320:384], wv[:, :, 640:704])
        cvt(l, 1, slab(1)[:, :, 384:448], wv[:, :, 704:768])
        cvt(l, 1, slab(1)[:, :, 448:512], wv[:, :, 704:768])
        cvt(l, 2, slab(2)[:, :, :], wv[:, :, 1792:2304])
        cvt(l, 3, slab(3)[:, :, :], wv[:, :, 1280:1792])
        cvt(l, 4, slab(4)[:, :, :], wv[:, :, 768:1280])
        cvt(l, 5, slab(5)[:, :, :], wv[:, :, 2304:2816])
        for i in range(4):
            cvt(l, 6 + i, slab(6 + i)[:, :, :], wv[:, :, 2816 + i * 512: 2816 + (i + 1) * 512])
        cvt(l, 10, ws[l][10].rearrange("p (k c) -> p k c", k=4), p_a[l].rearrange("(k p) c -> p k c", p=128))
        cvt(l, 11, ws[l][11].rearrange("p (k c) -> p k c", k=4), p_b[l].rearrange("(k p) c -> p k c", p=128))
        wo = w_out[l].rearrange("(k p) c -> p k c", p=128)
        for h in range(2):
            cvt(l, 12 + h, slab(12 + h)[:, :, :], wo[:, :, h * 512:(h + 1) * 512])

    def load_x(t, j):
        r0 = t * 512 + j * 128
        dma("sp", xres[:, j, :], x_d[r0:r0 + 128, :], (), (("xres", j),), chan=("xin", j))

    for j in range(4):
        load_x(0, j)

    CH = "const"

    def cdma(out, in_, key, eng="sp", slow=False, reads=(), chan=None):
        if chan is not None:
            dma(eng, out, in_, reads, (key,), chan=chan, batch=False, slow=slow)
        else:
            dma(eng, out, in_, reads, (key,), chan=CH, batch=True, slow=slow)

    P.add("pool", lambda e: e.memset(ones_bf[:], 1.0), (), (("ones_bf",),))
    P.add("pool", lambda e: e.memset(ones_f[:], 1.0), (), (("ones_f",),))
    P.add("pool", lambda e: e.memset(neghalf[:], -0.5), (), (("neghalf",),))
    P.add("pool", lambda e: e.memset(onesrow[:], 1.0), (), (("onesrow",),))
    P.add("pool", lambda e: e.memset(twosrow[:], 2.0), (), (("twosrow",),))
    P.add("pool", lambda e: e.memset(maskc[:], 1.0), (), (("maskc",),))
    P.add("pool", lambda e: e.memset(ident_bf[:], 1.0), (), (("ident_bf",),))
    P.add("pool", lambda e: e.memset(ident_f[:], 1.0), (), (("ident_f",),))
    P.add("pool", lambda e: e.memset(L2[:], 1.0), (), (("z", 1),))
    for l in layers:
        P.add("pool", (lambda l: lambda e: e.memset(kpad[l][:], 0.0))(l), (), (("kpadz", l),))
    P.add("pool", lambda e: e.affine_select(out=maskc[:, 0, :].rearrange("p (g t) -> p g t", g=4),
                                             in_=maskc[:, 0, :].rearrange("p (g t) -> p g t", g=4),
                                             pattern=[[0, 4], [-1, 128]], compare_op=ALU.is_gt, fill=0.0,
                                             base=0, channel_multiplier=1), (("maskc",),), (("maskc",),))
    P.add("pool", lambda e: e.affine_select(out=maskc[:, 1, :].rearrange("p (g t) -> p g t", g=4),
                                             in_=maskc[:, 1, :].rearrange("p (g t) -> p g t", g=4),
                                             pattern=[[0, 4], [1, 128]], compare_op=ALU.is_ge, fill=0.0,
                                             base=0, channel_multiplier=-1), (("maskc",),), (("maskc",),))
    for idt, key in ((ident_bf, "ident_bf"), (ident_f, "ident_f")):
        P.add("pool", (lambda idt: lambda e: e.affine_select(out=idt[:], in_=idt[:], pattern=[[1, 128]],
                                                             compare_op=ALU.is_equal, fill=0.0, base=0,
                                                             channel_multiplier=-1))(idt),
              ((key,),), ((key,),))
    for i, src in enumerate((lnin_g[0], lnin_b[0], ln_g[0], ln_b[0], ln_g[1], ln_b[1])):
        cdma(gb[:, i, :], src.partition_broadcast(128), ("gb", i))
    for i, src in enumerate((lnin_g[0], lnin_b[0], ln_g[0], ln_b[0], ln_g[1], ln_b[1])):
        dma("sp", gT[:, i, :], src.rearrange("(k p) -> p k", p=128), (), (("gT", i),), CH, True, True)
    cdma(sink_sb[:], sinks, ("sink_sb",))
    act(sinkexp[:], sink_sb[:], AF.Exp, (("sink_sb",),), (("sinkexp",),))
    for l in layers:
        bf = bias_fm[l]
        bl = b_in[l]

        def bcols(c0, n):
            return bl[c0:c0 + 128 * n].rearrange("(t p) -> p t", p=128)
        kb = ("bias_fm", l)
        dma("sp", bf[:, 0:4], bcols(0, 4), (), (kb + (0,),), CH, True, True)
        dma("sp", bf[:, 4:5], bcols(512, 1), (), (kb + (1,),), CH, True, True)
        dma("sp", bf[0:64, 5:6], bl[576:640].rearrange("(t p) -> p t", p=64), (), (kb + (2,),), CH, True, True)
        dma("sp", bf[64:128, 5:6], bl[512:576].rearrange("(t p) -> p t", p=64), (), (kb + (3,),), CH, True, True)
        dma("sp", bf[:, 6:10], bcols(1280, 4), (), (kb + (4,),), CH, True, True)
        dma("sp", bf[:, 10:14], bcols(768, 4), (), (kb + (5,),), CH, True, True)
        dma("sp", bf[:, 14:18], bcols(2304, 4), (), (kb + (6,),), CH, True, True)
        dma("sp", bf[:, 18:34], bcols(2816, 16), (), (kb + (7,),), CH, True, True)
        bkeys = tuple(kb + (i,) for i in range(8))
        P.add("dve", (lambda l: lambda e: e.tensor_scalar(out=halfb[l][:], in0=bias_fm[l][:, 18:34], scalar1=0.5,
                                                          scalar2=None, op0=ALU.mult))(l),
              bkeys, (("halfb", l),))
        br = brow[l]
        bl2 = b_in[l:l + 1, :]
        rk = ("brow", l)
        for i, (d0, s0, n) in enumerate(((0, 640, 64), (64, 640, 64), (128, 704, 64), (192, 704, 64), (256, 1792, 512))):
            dma("pool", br[0:1, d0:d0 + n], bl2[:, s0:s0 + n], (), (rk + (i,),), CH, True)
        dma("pool", br[0:1, 768:1792], b_out[l:l + 1, :], (), (rk + (5,),), CH, True)
        dma("sp", vng[l][:], vn_g[l].rearrange("(g p) -> p g", p=128), (), (("vng", l),), CH, True, True)
        for hk in range(2):
            for g in range(4):
                h = hk * 4 + g
                P.add("dve", (lambda l, hk, g, h: lambda e: e.tensor_scalar(
                    out=sinkrow[0:1, l * 2 + hk, g * 128:(g + 1) * 128], in0=ones_f[0:1, :],
                    scalar1=sinkexp[0:1, l * 8 + h:l * 8 + h + 1], scalar2=None, op0=ALU.mult))(l, hk, g, h),
                    (("sinkexp",), ("ones_f",)), (("sinkrow", l, hk, g),))
    for l in layers:
        cdma(wnat[:], w_s[l].rearrange("g t s -> t g s"), ("z", 0), chan="wl")
        bT = next_bank()
        for g in range(4):
            P.add("pe", (lambda g, bT: lambda e: e.transpose(out=ps[bT][:, g * 128:(g + 1) * 128], in_=wnat[:, g, :],
                                                             identity=ident_f[:]))(g, bT),
                  (("z", 0), ("ident_f",)), (("ps", bT),))
        P.add("dve", (lambda bT: lambda e: e.tensor_copy(out=wT_f[:].rearrange("p g t -> p (g t)"), in_=ps[bT][:]))(bT),
              (("ps", bT),), (("z", 0),))
        P.add("pool", lambda e: e.affine_select(out=wT_f[:], in_=wT_f[:], pattern=[[0, 4], [1, 128]],
                                                 compare_op=ALU.is_ge, fill=0.0, base=0, channel_multiplier=-1),
              (("z", 0),), (("z", 0),))
        P.add("dve", (lambda l: lambda e: e.tensor_copy(out=wT_bf[l][:], in_=wT_f[:]))(l), (("z", 0),), (("wT_bf", l),))
        bR = next_bank()
        P.add("pe", (lambda bR: lambda e: e.matmul(ps[bR][0:1, :], ones_f[:, 0:1], wT_f[:].rearrange("p g t -> p (g t)"),
                                                   start=True, stop=True))(bR),
              (("z", 0), ("ones_f",)), (("ps", bR),))
        P.add("dve", (lambda bR: lambda e: e.tensor_copy(out=R2[0:1, :], in_=ps[bR][0:1, :]))(bR),
              (("ps", bR),), (("z", 1),))
        cdma(R2[1:2, :], b_s[l:l + 1, :], ("z", 1), chan="r2l")
        cdma(L2[0:1, :], vn_b[l:l + 1, :], ("z", 1), reads=(("z", 1),), chan="l2l")
        bC = next_bank()
        for g in range(4):
            P.add("pe", (lambda g, bC: lambda e: e.matmul(ps[bC][:, g * 128:(g + 1) * 128], L2[0:2, g * 128:(g + 1) * 128],
                                                          R2[0:2, g * 128:(g + 1) * 128], start=True, stop=True))(g, bC),
                  (("z", 1), ("z", 1), ("z", 1)), (("ps", bC),))
        P.add("dve", (lambda l, bC: lambda e: e.tensor_copy(out=Cc[l][:].rearrange("p g t -> p (g t)"), in_=ps[bC][:]))(l, bC),
              (("ps", bC),), (("Cc", l),))

    for l in layers:
        setup_cvt(l)

    slab_seq = [(t, l, s) for t in range(NT) for l in layers for s in range(14)]
    wstate = {"next_load": 0, "cur": 0}

    def issue_load():
        n = wstate["next_load"]
        if n >= len(slab_seq):
            return
        wstate["next_load"] = n + 1
        _, l, s = slab_seq[n]
        slot = n % R_SLOTS
        dma("sp", ring[:, slot, :], ws[l][s], tuple(cvt_keys[(l, s)]), (("ring", slot),), chan=("ring", slot))

    def acquire(l, s):
        n = wstate["cur"]
        assert slab_seq[n][1] == l and slab_seq[n][2] == s, (slab_seq[n], l, s)
        slot = n % R_SLOTS
        return slot

    def release():
        wstate["cur"] += 1
        issue_load()

    def slab8(slot):
        return ring[:, slot, :].rearrange("p (k c) -> p k c", k=8)

    def slab4(slot):
        return ring[:, slot, :].rearrange("p (k c) -> p k c", k=4)

    XT_ALL = tuple(("xT", j) for j in range(4))

    def ln_p1(src, src_key, eps):
        si = rot("stat", 8)
        st = stats[:, si, :]
        skey = ("stat", si)
        P.add("dve", lambda e: e.bn_stats(out=st[:, 0:6], in_=src[:, 0:512]), (src_key,), (skey,))
        P.add("dve", lambda e: e.bn_stats(out=st[:, 6:12], in_=src[:, 512:1024]), (src_key, skey), (skey,))
        P.add("dve", lambda e: e.bn_aggr(out=mv[:, si, :], in_=st), (skey,), (("mv", si),))
        P.add("dve", lambda e: e.tensor_scalar(out=sm[:, si, 0:1], in0=mv[:, si, 1:2], scalar1=eps, scalar2=None,
                                               op0=ALU.add), (("mv", si),), (("sm0", si),))
        P.add("pool", lambda e: e.tensor_tensor(out=sm[:, si, 1:2], in0=sm[:, si, 0:1], in1=neghalf[:], op=ALU.pow),
              (("sm0", si), ("neghalf",)), (("sm1", si),))
        return si

    deferred_b = []
    deferred_c = []

    def flush_deferred(c=True):
        while deferred_b:
            deferred_b.pop(0)()
        while c and deferred_c:
            deferred_c.pop(0)()

    def ln_p2(si, src, src_key, dst, dst_key, gi, make_bf, after=None):
        P.add("dve", lambda e: e.tensor_scalar(out=sm[:, si, 2:3], in0=mv[:, si, 0:1], scalar1=-1.0,
                                               scalar2=sm[:, si, 1:2], op0=ALU.mult, op1=ALU.mult),
              (("mv", si), ("sm1", si)), (("sm2", si),))
        SK = (("sm1", si), ("sm2", si))
        if make_bf:
            j = make_bf - 1
            bi = rot("xbf", 4)
            act(xbf[:, bi, :], src, AF.Identity, (src_key,) + SK, (("xbf", bi),),
                bias=sm[:, si, 2:3], scale=sm[:, si, 1:2])

            def part_b():
                for kc in range(8):
                    P.add("pe", (lambda kc, bi: lambda e: e.transpose(out=psT[:, kc * 128:(kc + 1) * 128],
                                                                     in_=xbf[:, bi, kc * 128:(kc + 1) * 128],
                                                                     identity=ident_bf[:]))(kc, bi),
                          (("xbf", bi), ("ident_bf",)), (("psT",),))
                xo = xT[:, :, j * 128:(j + 1) * 128]
                P.add("dve", lambda e: e.tensor_tensor(out=xo, in0=psT[:].rearrange("p (k t) -> p k t", k=8),
                                                       in1=gT[:, gi, :].unsqueeze(2).to_broadcast([128, 8, 128]),
                                                       op=ALU.mult),
                      (("psT",), ("gT", gi)), (("xT", j),))
                P.add("dve", lambda e: e.tensor_tensor(out=xo, in0=xo,
                                                       in1=gT[:, gi + 1, :].unsqueeze(2).to_broadcast([128, 8, 128]),
                                                       op=ALU.add),
                      (("xT", j), ("gT", gi + 1)), (("xT", j),))
            deferred_b.append(part_b)
        act(dst, src, AF.Identity, (src_key,) + SK, (dst_key,), bias=sm[:, si, 2:3], scale=sm[:, si, 1:2])

        def part_c():
            P.add("pool", lambda e: e.tensor_tensor(out=dst, in0=dst, in1=gb[:, gi, :], op=ALU.mult),
                  (dst_key, ("gb", gi)), (dst_key,))
            P.add("pool", lambda e: e.tensor_tensor(out=dst, in0=dst, in1=gb[:, gi + 1, :], op=ALU.add),
                  (dst_key, ("gb", gi + 1)), (dst_key,))
            if after is not None:
                after()
        deferred_c.append(part_c)

    def to_xT(src, src_key, j):
        bi = rot("xbf", 4)
        act(xbf[:, bi, :], src, AF.Copy, (src_key,), (("xbf", bi),))
        for kc in range(8):
            P.add("pe", (lambda kc, bi: lambda e: e.transpose(out=psT[:, kc * 128:(kc + 1) * 128],
                                                             in_=xbf[:, bi, kc * 128:(kc + 1) * 128],
                                                             identity=ident_bf[:]))(kc, bi),
                  (("xbf", bi), ("ident_bf",)), (("psT",),))
        act(xT[:, :, j * 128:(j + 1) * 128], psT[:].rearrange("p (k t) -> p k t", k=8), AF.Copy,
            (("psT",),), (("xT", j),))

    lnin_si = {}

    def lnin_step(j):
        if j < 4:
            lnin_si[j] = ln_p1(xres[:, j, :], ("xres", j), EPS)
        if j >= 1:
            ln_p2(lnin_si[j - 1], xres[:, j - 1, :], ("xres", j - 1), xres[:, j - 1, :], ("xres", j - 1), 0, j)

    def layer(l, t, last):
        first_tile_of_seq = (t * 512) % seq_len == 0
        bfm = bias_fm[l]
        BK = tuple(("bias_fm", l, i) for i in range(8))

        def fm_tile(slot, col, out_fn, reads_extra=()):
            b = next_bank("A")
            sl = slab8(slot)
            for kc in range(8):
                mm(ps[b][:], sl[:, kc, col * 128:(col + 1) * 128], xT[:, kc, :], kc == 0, kc == 7,
                   (("ring", slot),) + XT_ALL, (("ps", b),))
            out_fn(b)

        slot = acquire(l, 0)
        for c in range(4):
            fm_tile(slot, c, (lambda c: lambda b: act(qT[:, c, :], ps[b][:], AF.Identity, (("ps", b),) + BK,
                                                      (("qT", c),), bias=bfm[:, c:c + 1]))(c))
        release()
        slot = acquire(l, 1)
        kp = kpad[l]

        def k_out(b, col, var_lo, var_hi):
            act(kp[0:64, var_lo, 128:640], ps[b][0:64, :], AF.Identity, (("ps", b), ("kpadz", l)) + BK,
                (("kpad", l, var_lo),), bias=bfm[0:64, col:col + 1])
            act(kp[64:128, var_hi, 128:640], ps[b][64:128, :], AF.Identity, (("ps", b), ("kpadz", l)) + BK,
                (("kpad", l, var_hi),), bias=bfm[64:128, col:col + 1])
        fm_tile(slot, 0, lambda b: k_out(b, 4, 0, 3))
        fm_tile(slot, 1, lambda b: k_out(b, 5, 2, 1))
        sl = slab8(slot)
        RK = tuple(("brow", l, i) for i in range(6))
        for j in range(4):
            b = next_bank()
            for kc in range(8):
                mm(ps[b][:, 0:256], xT[:, kc, j * 128:(j + 1) * 128], sl[:, kc, 256:512], kc == 0, False,
                   (("ring", slot), ("xT", j)), (("ps", b),))
            mm(ps[b][:, 0:256], onesrow[0:1, :], brow[l][0:1, 0:256], False, True, RK + (("onesrow",),), (("ps", b),))
            act(vdup[l][:, 1 + j, :], ps[b][:, 0:256], AF.Copy, (("ps", b),), (("vdup", l, 1 + j),))
        release()
        flush_deferred()
        slot = acquire(l, 2)
        sl = slab8(slot)
        for j in range(4):
            b = next_bank()
            for kc in range(8):
                mm(ps[b][:], xT[:, kc, j * 128:(j + 1) * 128], sl[:, kc, :], kc == 0, False,
                   (("ring", slot), ("xT", j)), (("ps", b),))
            mm(ps[b][:], onesrow[0:1, :], brow[l][0:1, 256:768], False, True, RK + (("onesrow",),), (("ps", b),))
            vi = rot("vbg", 2)
            act(vbg[:, vi, :], ps[b][:], AF.Gelu_apprx_tanh, (("ps", b),), (("vbg", vi),))
            si = rot("stat", 8)
            P.add("dve", (lambda vi, si: lambda e: e.bn_stats(out=stats[:, si, 0:6], in_=vbg[:, vi, :]))(vi, si),
                  (("vbg", vi),), (("stat", si),))
            P.add("dve", (lambda si: lambda e: e.bn_aggr(out=mv[:, si, :], in_=stats[:, si, 0:6]))(si),
                  (("stat", si),), (("mv", si),))
            P.add("dve", (lambda si: lambda e: e.tensor_scalar(out=sm[:, si, 0:1], in0=mv[:, si, 1:2], scalar1=EPS,
                                                               scalar2=None, op0=ALU.add))(si),
                  (("mv", si),), (("sm0", si),))
            P.add("pool", (lambda si: lambda e: e.tensor_tensor(out=sm[:, si, 1:2], in0=sm[:, si, 0:1], in1=neghalf[:],
                                                                op=ALU.pow))(si),
                  (("sm0", si), ("neghalf",)), (("sm1", si),))
            P.add("dve", (lambda vi, si, j: lambda e: e.tensor_scalar(out=vnh[:, j, :], in0=vbg[:, vi, :],
                                                                      scalar1=mv[:, si, 0:1], scalar2=sm[:, si, 1:2],
                                                                      op0=ALU.subtract, op1=ALU.mult))(vi, si, j),
                  (("vbg", vi), ("mv", si), ("sm1", si)), (("vnh", j),))
        release()
        slot = acquire(l, 3)
        for c in range(4):
            fm_tile(slot, c, (lambda c: lambda b: act(uT[:, c, :], ps[b][:], AF.Gelu_apprx_tanh, (("ps", b),) + BK,
                                                      (("uT", c),), bias=bfm[:, 6 + c:7 + c]))(c))
        release()
        slot = acquire(l, 4)
        for c in range(4):
            fm_tile(slot, c, (lambda c: lambda b: act(gaT[:, c, :], ps[b][:], AF.Silu, (("ps", b),) + BK,
                                                      (("gaT", c),), bias=bfm[:, 10 + c:11 + c]))(c))
        release()
        slot = acquire(l, 5)
        for c in range(4):
            fm_tile(slot, c, (lambda c: lambda b: act(gbT[:, c, :], ps[b][:], AF.Silu, (("ps", b),) + BK,
                                                      (("gbT", c),) + tuple(("ybT", jj) for jj in range(4)),
                                                      bias=bfm[:, 14 + c:15 + c]))(c))
        release()
        for c in range(4):
            P.add("pool", (lambda c: lambda e: e.tensor_tensor(out=uT[:, c, :], in0=uT[:, c, :], in1=gbT[:, c, :],
                                                               op=ALU.mult))(c),
                  (("uT", c), ("gbT", c)), (("uT", c),))

        def attn_scores(j, hk):
            first = first_tile_of_seq and j == 0
            kbs = [1] if first else [0, 1]
            pts = []
            for kbi in kbs:
                cb = j + kbi
                bS = next_bank("B")
                for g in range(4):
                    var = hk * 2 + (g % 2)
                    mm(ps[bS][:, g * 128:(g + 1) * 128], kp[:, var, cb * 128:(cb + 1) * 128],
                       qT[:, 2 * hk + g // 2, j * 128:(j + 1) * 128], True, True,
                       (("kpad", l, var), ("qT", 2 * hk + g // 2), ("kpadz", l)), (("ps", bS),))
                pi = rot("PT", 4)
                act(PT[:, pi, :], ps[bS][:], AF.Exp, (("ps", bS),), (("PT", pi),), scale=0.125)
                P.add("dve", (lambda pi, kbi: lambda e: e.tensor_tensor(out=PT[:, pi, :], in0=PT[:, pi, :],
                                                                       in1=maskc[:, kbi, :], op=ALU.mult))(pi, kbi),
                      (("PT", pi), ("maskc",)), (("PT", pi),))
                pts.append((pi, cb))
            return pts

        def attn_pv(j, hk, pts):
            def gsel(ap, b):
                return ap.rearrange("p (a b t) -> p a b t", a=2, b=2)[:, :, b, :]
            bO = next_bank("C")
            for b in range(2):
                for i, (pi, cb) in enumerate(pts):
                    mm(ps[bO][b * 64:(b + 1) * 64, 0:256].rearrange("p (a t) -> p a t", a=2),
                       vdup[l][:, cb, hk * 128:hk * 128 + 64], gsel(PT[:, pi, :], b), i == 0, i == len(pts) - 1,
                       (("vdup", l, cb), ("PT", pi)), (("ps", bO),))
            bD = next_bank("C")
            for b in range(2):
                for i, (pi, cb) in enumerate(pts):
                    mm(ps[bD][b * 64:(b + 1) * 64, 0:256].rearrange("p (a t) -> p a t", a=2),
                       ones_bf[:, 0:64], gsel(PT[:, pi, :], b), i == 0, False,
                       (("ones_bf",), ("PT", pi)), (("ps", bD),))
                mm(ps[bD][b * 64:(b + 1) * 64, 0:256].rearrange("p (a t) -> p a t", a=2),
                   onesrow[0:1, 0:64], gsel(sinkrow[0:1, l * 2 + hk, :], b), False, True,
                   tuple(("sinkrow", l, hk, g) for g in range(4)) + (("onesrow",),), (("ps", bD),))
            ri = rot("rec", 1)
            P.add("dve", (lambda ri, bD: lambda e: e.reciprocal(out=rec[:, ri, 0:256], in_=ps[bD][:, 0:256]))(ri, bD),
                  (("ps", bD),), (("rec", ri),))
            ti = rot("t1", 2)
            P.add("dve", (lambda ti, ri, bO: lambda e: e.tensor_tensor(
                out=t1[:, ti, :], in0=ps[bO][:, 0:256], in1=rec[:, ri, 0:256], op=ALU.mult))(ti, ri, bO),
                (("ps", bO), ("rec", ri)), (("t1", ti),))
            P.add("dve", (lambda ti, hk, j: lambda e: e.tensor_tensor(
                out=yaT[:, 2 * hk:2 * hk + 2, j * 128:(j + 1) * 128],
                in0=t1[:, ti, :].rearrange("p (a t) -> p a t", a=2),
                in1=gaT[:, 2 * hk:2 * hk + 2, j * 128:(j + 1) * 128], op=ALU.mult))(ti, hk, j),
                (("t1", ti), ("gaT", 2 * hk), ("gaT", 2 * hk + 1)), (("yaT", j, hk),))

        def sgu_block(j):
            bM = next_bank("C")
            for g in range(4):
                mm(ps[bM][:, g * 128:(g + 1) * 128], vnh[:, j, g * 128:(g + 1) * 128], wT_bf[l][:, g, :], True, True,
                   (("vnh", j), ("wT_bf", l)), (("ps", bM),))
            mi = rot("mixed", 1)
            for g in range(4):
                P.add("dve", (lambda g, mi, bM: lambda e: e.scalar_tensor_tensor(
                    out=mixed[:, mi, g * 128:(g + 1) * 128], in0=ps[bM][:, g * 128:(g + 1) * 128],
                    scalar=vng[l][:, g:g + 1], in1=Cc[l][:, g, :], op0=ALU.mult, op1=ALU.add))(g, mi, bM),
                    (("ps", bM), ("vng", l), ("Cc", l), ("mixed", mi)), (("mixed", mi),))
            P.add("dve", (lambda mi, j: lambda e: e.tensor_tensor(
                out=ybT[:, :, j * 128:(j + 1) * 128], in0=mixed[:, mi, :].rearrange("p (g t) -> p g t", g=4),
                in1=uT[:, :, j * 128:(j + 1) * 128], op=ALU.mult))(mi, j),
                (("mixed", mi),) + tuple(("uT", c) for c in range(4)), (("ybT", j),) + tuple(("gbT", c) for c in range(4)))

        units = [(j, hk) for j in range(4) for hk in range(2)]
        pend = {}
        ti_ = 0
        for i in range(4):
            slot = acquire(l, 6 + i)
            dst = tRa if i < 2 else tRb
            for c in range(4):
                m = (i % 2) * 4 + c
                col = (i * 4 + c)
                fm_tile(slot, c, (lambda dst, m, col: lambda b: act(dst[:, m, :], ps[b][:], AF.Tanh,
                                                                    (("ps", b), ("halfb", l)),
                                                                    ((dst is tRa and "tRa" or "tRb", m),),
                                                                    bias=halfb[l][:, col:col + 1], scale=0.5))(dst, m, col))
                if ti_ % 2 == 0:
                    u = ti_ // 2
                    if u - 1 in pend:
                        attn_pv(units[u - 1][0], units[u - 1][1], pend.pop(u - 1))
                    pend[u] = attn_scores(*units[u])
                elif ti_ // 2 < 4:
                    sgu_block(ti_ // 2)
                ti_ += 1
            release()
        attn_pv(units[7][0], units[7][1], pend.pop(7))
        P.add("pool", lambda e: e.tensor_copy(out=kp[:, :, 0:128], in_=kp[:, :, 512:640]),
              tuple(("kpad", l, v) for v in range(4)), tuple(("kpad", l, v) for v in range(4)))
        P.add("pool", lambda e: e.tensor_copy(out=vdup[l][:, 0, :], in_=vdup[l][:, 4, :]),
              (("vdup", l, 4),), (("vdup", l, 0),))

        sa = acquire(l, 10)
        sA = slab4(sa)
        sbn = (wstate["cur"] + 1) % R_SLOTS
        YA = tuple(("yaT", j, hk) for j in range(4) for hk in range(2))
        YB = tuple(("ybT", j) for j in range(4))
        sB = slab4(sbn)
        for m in range(8):
            bA = next_bank()
            for kc in range(4):
                mm(ps[bA][:], sA[:, kc, m * 128:(m + 1) * 128], yaT[:, kc, :], kc == 0, kc == 3,
                   (("ring", sa),) + YA, (("ps", bA),))
            bB = next_bank()
            for kc in range(4):
                mm(ps[bB][:], sB[:, kc, m * 128:(m + 1) * 128], ybT[:, kc, :], kc == 0, kc == 3,
                   (("ring", sbn),) + YB, (("ps", bB),))
            ai = rot("Ap", 2)
            P.add("dve", (lambda ai, bA, m: lambda e: e.scalar_tensor_tensor(
                out=Ap[:, ai, :], in0=tRa[:, m, :], scalar=1.0, in1=ps[bA][:], op0=ALU.add, op1=ALU.mult))(ai, bA, m),
                (("tRa", m), ("ps", bA)), (("Ap", ai),))
            P.add("dve", (lambda ai, bB, m: lambda e: e.scalar_tensor_tensor(
                out=Bp[:, ai, :], in0=tRb[:, m, :], scalar=1.0, in1=ps[bB][:], op0=ALU.add, op1=ALU.mult))(ai, bB, m),
                (("tRb", m), ("ps", bB)), (("Bp", ai),))
            P.add("pool", (lambda ai, m: lambda e: e.tensor_tensor(out=mT[:, m, :], in0=Ap[:, ai, :], in1=Bp[:, ai, :],
                                                                   op=ALU.add))(ai, m),
                  (("Ap", ai), ("Bp", ai)), (("mT", m),))
        release()
        s2 = acquire(l, 11)
        assert s2 == sbn
        release()
        def ln_block2(j, zi, si):
            if last:
                r0 = t * 512 + j * 128

                def out_dma():
                    dma("sp", y_d[r0:r0 + 128, :], zbuf[:, zi, :], (("z", zi),), (("y", t, j),), chan=("out", zi))
                ln_p2(si, zbuf[:, zi, :], ("z", zi), zbuf[:, zi, :], ("z", zi), 2 + 2 * l, 0, after=out_dma)
            else:
                ln_p2(si, zbuf[:, zi, :], ("z", zi), xres[:, j, :], ("xres", j), 2 + 2 * l, j + 1)

        so0 = acquire(l, 12)
        so1 = (wstate["cur"] + 1) % R_SLOTS
        MT = tuple(("mT", m) for m in range(8))
        zis = []
        sis = []
        for j in range(4):
            zi = rot("z", 4)
            zis.append(zi)
            for h, so in ((0, so0), (1, so1)):
                b = next_bank()
                sl = slab8(so)
                for kc in range(8):
                    mm(ps[b][:], mT[:, kc, j * 128:(j + 1) * 128], sl[:, kc, :], kc == 0, False,
                       (("ring", so),) + MT, (("ps", b),))
                mm(ps[b][:], twosrow[0:1, :], brow[l][0:1, 768 + h * 512:768 + (h + 1) * 512], False, True,
                   RK + (("twosrow",),), (("ps", b),))
                P.add("dve", (lambda zi, h, b, j: lambda e: e.scalar_tensor_tensor(
                    out=zbuf[:, zi, h * 512:(h + 1) * 512], in0=xres[:, j, h * 512:(h + 1) * 512], scalar=2.0 * ALPHA,
                    in1=ps[b][:], op0=ALU.mult, op1=ALU.add))(zi, h, b, j),
                    (("xres", j), ("ps", b), ("z", zi)), (("z", zi),))
            nxt = last and t + 1 < NT
            if nxt:
                load_x(t + 1, j)
            sis.append(ln_p1(zbuf[:, zi, :], ("z", zi), 4.0 * EPS))
            if j >= 1:
                ln_block2(j - 1, zis[j - 1], sis[j - 1])
            if not last and j >= 2:
                deferred_b.pop(0)()
            if nxt and do_ln_in and j >= 1:
                lnin_step(j - 1)
        ln_block2(3, zis[3], sis[3])
        if last and t + 1 < NT and do_ln_in:
            lnin_step(3)
            lnin_step(4)
        elif last and t + 1 < NT:
            for jj in range(4):
                to_xT(xres[:, jj, :], ("xres", jj), jj)
        flush_deferred(c=not (last and t + 1 < NT))
        release()
        s3 = acquire(l, 13)
        assert s3 == so1
        release()

    for _ in range(R_SLOTS):
        issue_load()
    for t in range(NT):
        if t == 0:
            if do_ln_in:
                for jj in range(5):
                    lnin_step(jj)
            else:
                for jj in range(4):
                    to_xT(xres[:, jj, :], ("xres", jj), jj)
            flush_deferred()
        for li, l in enumerate(layers):
            layer(l, t, li == len(layers) - 1)

    flush_deferred()
    P.emit(nc, es)
    es.close()
    return nc


_CACHE = {}


def _get_nc(n_tok, seq_len, layers, do_ln_in):
    key = (n_tok, seq_len, tuple(layers), do_ln_in)
    if key not in _CACHE:
        _CACHE[key] = build(n_tok, seq_len, layers, do_ln_in)
    return _CACHE[key]


def _run(x, params, layers=(0, 1), do_ln_in=True, n_cores=N_CORES):
    B, S, _ = x.shape
    per = B // n_cores
    n_tok = per * S
    nc = _get_nc(n_tok, S, layers, do_ln_in)
    f = lambda a: np.ascontiguousarray(np.asarray(a, dtype=np.float32))
    shared = {
        "ln_in_g": f(params["ln_in_g"]).reshape(1, D), "ln_in_b": f(params["ln_in_b"]).reshape(1, D),
        "w_in": f(params["w_in"]), "b_in": f(params["b_in"]), "sinks": f(params["sinks"]).reshape(1, 16),
        "vn_g": f(params["vn_g"]), "vn_b": f(params["vn_b"]), "w_s": f(params["w_s"]),
        "b_s": f(params["b_s"]).reshape(2, 512), "p_a": f(params["p_a"]), "p_b": f(params["p_b"]),
        "w_out": f(params["w_out"]), "b_out": f(params["b_out"]), "ln_g": f(params["ln_g"]), "ln_b": f(params["ln_b"]),
    }
    xf = f(x).reshape(n_cores, n_tok, D)
    in_maps = [dict(shared, x=xf[c]) for c in range(n_cores)]
    res = run_bass_kernel_spmd(nc, in_maps, core_ids=list(range(n_cores)))
    out = np.stack([np.asarray(r["y"]) for r in res.results], axis=0)
    return out.reshape(B, S, D).astype(np.float32)


def kernel(x, ln_in_g, ln_in_b, w_in, b_in, sinks, vn_g, vn_b, w_s, b_s, p_a, p_b, w_out, b_out, ln_g, ln_b):
    params = dict(ln_in_g=ln_in_g, ln_in_b=ln_in_b, w_in=w_in, b_in=b_in, sinks=sinks, vn_g=vn_g, vn_b=vn_b,
                  w_s=w_s, b_s=b_s, p_a=p_a, p_b=p_b, w_out=w_out, b_out=b_out, ln_g=ln_g, ln_b=ln_b)
    return _run(np.asarray(x), params)
```

```python
import contextlib
import numpy as np
import concourse.bass as bass
import concourse.mybir as mybir
from concourse.bass_utils import run_bass_kernel_spmd

F32 = mybir.dt.float32
BF16 = mybir.dt.bfloat16
AF = mybir.ActivationFunctionType
ALU = mybir.AluOpType

D = 1024
INC = 4864
ALPHA = 4.0 ** 0.25
EPS = 1e-5
R_SLOTS = 4
N_CORES = 8


class _Op:
    __slots__ = ("eng", "fn", "reads", "writes", "chan", "deps", "signal", "ticket", "idx")

    def __init__(self, eng, fn, reads, writes, chan):
        self.eng, self.fn, self.reads, self.writes, self.chan = eng, fn, reads, writes, chan
        self.deps = []
        self.signal = False
        self.ticket = 0


class Prog:
    ENGS = ("pe", "act", "dve", "pool", "sp")

    def __init__(self):
        self.ops = []
        self.last_w = {}
        self.readers = {}
        self.chan_count = {}
        self.chan_batch = {}

    def add(self, eng, fn, reads=(), writes=(), chan=None, batch=False):
        op = _Op(eng, fn, tuple(reads), tuple(writes), chan)
        deps = set()
        for k in op.reads:
            w = self.last_w.get(k)
            if w is not None:
                deps.add(w)
        for k in op.writes:
            w = self.last_w.get(k)
            if w is not None:
                deps.add(w)
            for r in self.readers.get(k, ()):
                deps.add(r)
        deps.discard(op)
        op.deps = list(deps)
        for k in op.reads:
            self.readers.setdefault(k, []).append(op)
        for k in op.writes:
            self.last_w[k] = op
            self.readers[k] = []
        if chan is not None:
            self.chan_count[chan] = self.chan_count.get(chan, 0) + 1
            op.ticket = self.chan_count[chan]
            self.chan_batch[chan] = batch
        op.idx = len(self.ops)
        self.ops.append(op)
        return op

    def emit(self, nc, es):
        for op in self.ops:
            for d in op.deps:
                if d.chan is None:
                    if d.eng == "pe" and op.eng == "pe" and op.chan is None:
                        continue
                    d.signal = True
        cnt = {e: 0 for e in self.ENGS}
        for op in self.ops:
            if op.chan is None and op.signal:
                cnt[op.eng] += 1
                op.ticket = cnt[op.eng]
        esem = {e: es.enter_context(nc.semaphore("e_" + e)) for e in self.ENGS if cnt[e] > 0}
        csem = {c: es.enter_context(nc.semaphore("c_%d" % i)) for i, c in enumerate(self.chan_count)}

        def event(d):
            if d.chan is not None:
                n = self.chan_count[d.chan] if self.chan_batch[d.chan] else d.ticket
                return csem[d.chan], 16 * n
            return esem[d.eng], d.ticket

        per_eng = {e: [] for e in self.ENGS}
        for op in self.ops:
            per_eng[op.eng].append(op)

        def run(ename, e):
            seen = {}
            for op in per_eng[ename]:
                waits = {}
                for d in op.deps:
                    if d.chan is None and d.eng == "pe" and ename == "pe" and op.chan is None:
                        continue
                    s, v = event(d)
                    if v > waits.get(s, (None, 0))[1]:
                        waits[s] = (s, v)
                if op.chan is not None and not self.chan_batch[op.chan] and op.ticket > 1:
                    s = csem[op.chan]
                    v = 16 * (op.ticket - 1)
                    if v > waits.get(s, (None, 0))[1]:
                        waits[s] = (s, v)
                for s, v in waits.values():
                    if seen.get(s, 0) < v:
                        e.wait_ge(s, v)
                        seen[s] = v
                ins = op.fn(e)
                if op.chan is not None:
                    ins.then_inc(csem[op.chan], 16)
                elif op.signal:
                    ins.then_inc(esem[ename], 1)
            if ename == "sp":
                for c, s in csem.items():
                    v = 16 * self.chan_count[c]
                    if seen.get(s, 0) < v:
                        e.wait_ge(s, v)
                        seen[s] = v

        with nc.Block() as block:
            @block.tensor
            def _(e):
                run("pe", e)

            @block.scalar
            def _(e):
                run("act", e)

            @block.vector
            def _(e):
                run("dve", e)

            @block.gpsimd
            def _(e):
                run("pool", e)

            @block.sync
            def _(e):
                run("sp", e)


def build(n_tok, seq_len, layers=(0, 1), do_ln_in=True):
    assert n_tok % 512 == 0 and seq_len % 512 == 0
    NT = n_tok // 512
    nc = bass.Bass("TRN2", target_bir_lowering=False)
    es = contextlib.ExitStack()
    P = Prog()

    def dram(name, shape, dt=F32, kind="ExternalInput"):
        return nc.dram_tensor(name, list(shape), dt, kind=kind).ap()

    x_d = dram("x", [n_tok, D])
    y_d = dram("y", [n_tok, D], kind="ExternalOutput")
    lnin_g = dram("ln_in_g", [1, D])
    lnin_b = dram("ln_in_b", [1, D])
    w_in = dram("w_in", [2, D, INC])
    b_in = dram("b_in", [2, INC])
    sinks = dram("sinks", [1, 16])
    vn_g = dram("vn_g", [2, 512])
    vn_b = dram("vn_b", [2, 512])
    w_s = dram("w_s", [2, 4, 128, 128])
    b_s = dram("b_s", [2, 512])
    p_a = dram("p_a", [2, 512, D])
    p_b = dram("p_b", [2, 512, D])
    w_out = dram("w_out", [2, D, D])
    b_out = dram("b_out", [2, D])
    ln_g = dram("ln_g", [2, D])
    ln_b = dram("ln_b", [2, D])
    ws = [dram("ws%d" % l, [14, 128, 4096], BF16, kind="Internal") for l in range(2)]

    def sb(name, shape, dt=F32):
        return es.enter_context(nc.sbuf_tensor(name, list(shape), dt))

    def psum(name, shape, dt=F32):
        return es.enter_context(nc.psum_tensor(name, list(shape), dt))

    ring = sb("ring", [128, R_SLOTS, 4096], BF16)
    xres = sb("xres", [128, 4, D])
    zbuf = sb("zbuf", [128, 4, D])
    xbf = sb("xbf", [128, 4, D], BF16)
    xT = sb("xT", [128, 8, 512], BF16)
    qT = sb("qT", [128, 4, 512], BF16)
    kpad = [sb("kpad%d" % l, [128, 4, 640], BF16) for l in range(2)]
    vdup = [sb("vdup%d" % l, [128, 5, 256], BF16) for l in range(2)]
    gaT = sb("gaT", [128, 4, 512], BF16)
    uT = sb("uT", [128, 4, 512], BF16)
    gbT = sb("gbT", [128, 4, 512], BF16)
    tRa = sb("tRa", [128, 8, 512], BF16)
    tRb = sb("tRb", [128, 8, 512], BF16)
    vbg = sb("vbg", [128, 2, 512])
    vnh = sb("vnh", [128, 4, 512], BF16)
    PT = sb("PT", [128, 4, 512], BF16)
    rec = sb("rec", [128, 1, 512])
    t1 = sb("t1", [128, 2, 256])
    yaT = sb("yaT", [128, 4, 512], BF16)
    ybT = gbT
    mixed = sb("mixed", [128, 1, 512])
    mT = sb("mT", [128, 8, 512], BF16)
    Ap = sb("Ap", [128, 2, 512], BF16)
    Bp = sb("Bp", [128, 2, 512], BF16)
    stats = sb("stats", [128, 8, 12])
    mv = sb("mv", [128, 8, 2])
    sm = sb("sm", [128, 8, 4])
    maskc = sb("maskc", [128, 2, 512], BF16)
    ident_bf = sb("ident_bf", [128, 128], BF16)
    ident_f = sb("ident_f", [128, 128])
    ones_bf = sb("ones_bf", [128, 128], BF16)
    ones_f = sb("ones_f", [128, 128])
    neghalf = sb("neghalf", [128, 1])
    onesrow = sb("onesrow", [1, 128], BF16)
    twosrow = sb("twosrow", [1, 128], BF16)
    gb = sb("gb", [128, 6, D])
    gT = sb("gT", [128, 6, 8])
    bias_fm = [sb("bias_fm%d" % l, [128, 34]) for l in range(2)]
    halfb = [sb("halfb%d" % l, [128, 16]) for l in range(2)]
    brow = [sb("brow%d" % l, [1, 1792], BF16) for l in range(2)]
    vng = [sb("vng%d" % l, [128, 4]) for l in range(2)]
    wT_bf = [sb("wTbf%d" % l, [128, 4, 128], BF16) for l in range(2)]
    wnat = zbuf[:, 0, 0:512].rearrange("p (g t) -> p g t", g=4)
    wT_f = zbuf[:, 0, 512:1024].rearrange("p (g t) -> p g t", g=4)
    Cc = [sb("Cc%d" % l, [128, 4, 128]) for l in range(2)]
    L2 = zbuf[0:2, 1, 0:512]
    R2 = zbuf[0:2, 1, 512:1024]
    sink_sb = sb("sink_sb", [1, 16])
    sinkexp = sb("sinkexp", [1, 16])
    sinkrow = sb("sinkrow", [1, 4, 512], BF16)

    NB = 7
    ps = [psum("ps%d" % i, [128, 512]) for i in range(NB)]
    psT = psum("psT", [128, 1024], BF16)
    bank_ctr = {}
    POOLS = {"all": list(range(NB)), "A": [0, 1], "B": [2, 3], "C": [4, 5, 6]}

    def next_bank(pool="all"):
        lst = POOLS[pool]
        c = bank_ctr.get(pool, 0)
        bank_ctr[pool] = c + 1
        return lst[c % len(lst)]

    cnt = {}

    def rot(name, n):
        v = cnt.get(name, 0)
        cnt[name] = v + 1
        return v % n

    def mm(out, lhsT, rhs, start, stop, reads, writes):
        P.add("pe", lambda e: e.matmul(out, lhsT, rhs, start=start, stop=stop), reads, writes)

    def act(out, in_, func, reads, writes, bias=None, scale=None):
        kw = {}
        if bias is not None:
            kw["bias"] = bias
        if scale is not None:
            kw["scale"] = scale
        P.add("act", lambda e: e.activation(out=out, in_=in_, func=func, **kw), reads, writes)

    def dma(eng, out, in_, reads, writes, chan, batch=False, slow=False):
        if chan == "const":
            chan = "const_" + eng
        if slow:
            P.add(eng, lambda e: e.dma_start(out=out, in_=in_, allow_slow_non_contiguous=True),
                  reads, writes, chan=chan, batch=batch)
        else:
            P.add(eng, lambda e: e.dma_start(out=out, in_=in_), reads, writes, chan=chan, batch=batch)

    cvt_keys = {}

    def cvt(l, s, dst, src):
        key = ("cvt", l, s, len(cvt_keys.setdefault((l, s), [])))
        cvt_keys[(l, s)].append(key)
        dma("pool", dst, src, (), (key,), chan=("cvt", l, s), batch=True)

    def setup_cvt(l):
        wv = w_in[l].rearrange("(k p) c -> p k c", p=128)

        def slab(s):
            return ws[l][s].rearrange("p (k c) -> p k c", k=8)
        cvt(l, 0, slab(0)[:, :, :], wv[:, :, 0:512])
        cvt(l, 1, slab(1)[:, :, 0:128], wv[:, :, 512:640])
        cvt(l, 1, slab(1)[:, :, 128:192], wv[:, :, 576:640])
        cvt(l, 1, slab(1)[:, :, 192:256], wv[:, :, 512:576])
        cvt(l, 1, slab(1)[:, :, 256:320], wv[:, :, 640:704])
        cvt(l, 1, slab(1)[:, :, 320:384], wv[:, :, 640:704])
        cvt(l, 1, slab(1)[:, :, 384:448], wv[:, :, 704:768])
        cvt(l, 1, slab(1)[:, :, 448:512], wv[:, :, 704:768])
        cvt(l, 2, slab(2)[:, :, :], wv[:, :, 1792:2304])
        cvt(l, 3, slab(3)[:, :, :], wv[:, :, 1280:1792])
        cvt(l, 4, slab(4)[:, :, :], wv[:, :, 768:1280])
        cvt(l, 5, slab(5)[:, :, :], wv[:, :, 2304:2816])
        for i in range(4):
            cvt(l, 6 + i, slab(6 + i)[:, :, :], wv[:, :, 2816 + i * 512: 2816 + (i + 1) * 512])
        cvt(l, 10, ws[l][10].rearrange("p (k c) -> p k c", k=4), p_a[l].rearrange("(k p) c -> p k c", p=128))
        cvt(l, 11, ws[l][11].rearrange("p (k c) -> p k c", k=4), p_b[l].rearrange("(k p) c -> p k c", p=128))
        wo = w_out[l].rearrange("(k p) c -> p k c", p=128)
        for h in range(2):
            cvt(l, 12 + h, slab(12 + h)[:, :, :], wo[:, :, h * 512:(h + 1) * 512])

    def load_x(t, j):
        r0 = t * 512 + j * 128
        dma("sp", xres[:, j, :], x_d[r0:r0 + 128, :], (), (("xres", j),), chan=("xin", j))

    for j in range(4):
        load_x(0, j)

    CH = "const"

    def cdma(out, in_, key, eng="sp", slow=False, reads=(), chan=None):
        if chan is not None:
            dma(eng, out, in_, reads, (key,), chan=chan, batch=False, slow=slow)
        else:
            dma(eng, out, in_, reads, (key,), chan=CH, batch=True, slow=slow)

    P.add("pool", lambda e: e.memset(ones_bf[:], 1.0), (), (("ones_bf",),))
    P.add("pool", lambda e: e.memset(ones_f[:], 1.0), (), (("ones_f",),))
    P.add("pool", lambda e: e.memset(neghalf[:], -0.5), (), (("neghalf",),))
    P.add("pool", lambda e: e.memset(onesrow[:], 1.0), (), (("onesrow",),))
    P.add("pool", lambda e: e.memset(twosrow[:], 2.0), (), (("twosrow",),))
    P.add("pool", lambda e: e.memset(maskc[:], 1.0), (), (("maskc",),))
    P.add("pool", lambda e: e.memset(ident_bf[:], 1.0), (), (("ident_bf",),))
    P.add("pool", lambda e: e.memset(ident_f[:], 1.0), (), (("ident_f",),))
    P.add("pool", lambda e: e.memset(L2[:], 1.0), (), (("z", 1),))
    for l in layers:
        P.add("pool", (lambda l: lambda e: e.memset(kpad[l][:], 0.0))(l), (), (("kpadz", l),))
    P.add("pool", lambda e: e.affine_select(out=maskc[:, 0, :].rearrange("p (g t) -> p g t", g=4),
                                             in_=maskc[:, 0, :].rearrange("p (g t) -> p g t", g=4),
                                             pattern=[[0, 4], [-1, 128]], compare_op=ALU.is_gt, fill=0.0,
                                             base=0, channel_multiplier=1), (("maskc",),), (("maskc",),))
    P.add("pool", lambda e: e.affine_select(out=maskc[:, 1, :].rearrange("p (g t) -> p g t", g=4),
                                             in_=maskc[:, 1, :].rearrange("p (g t) -> p g t", g=4),
                                             pattern=[[0, 4], [1, 128]], compare_op=ALU.is_ge, fill=0.0,
                                             base=0, channel_multiplier=-1), (("maskc",),), (("maskc",),))
    for idt, key in ((ident_bf, "ident_bf"), (ident_f, "ident_f")):
        P.add("pool", (lambda idt: lambda e: e.affine_select(out=idt[:], in_=idt[:], pattern=[[1, 128]],
                                                             compare_op=ALU.is_equal, fill=0.0, base=0,
                                                             channel_multiplier=-1))(idt),
              ((key,),), ((key,),))
    for i, src in enumerate((lnin_g[0], lnin_b[0], ln_g[0], ln_b[0], ln_g[1], ln_b[1])):
        cdma(gb[:, i, :], src.partition_broadcast(128), ("gb", i))
    for i, src in enumerate((lnin_g[0], lnin_b[0], ln_g[0], ln_b[0], ln_g[1], ln_b[1])):
        dma("sp", gT[:, i, :], src.rearrange("(k p) -> p k", p=128), (), (("gT", i),), CH, True, True)
    cdma(sink_sb[:], sinks, ("sink_sb",))
    act(sinkexp[:], sink_sb[:], AF.Exp, (("sink_sb",),), (("sinkexp",),))
    for l in layers:
        bf = bias_fm[l]
        bl = b_in[l]

        def bcols(c0, n):
            return bl[c0:c0 + 128 * n].rearrange("(t p) -> p t", p=128)
        kb = ("bias_fm", l)
        dma("sp", bf[:, 0:4], bcols(0, 4), (), (kb + (0,),), CH, True, True)
        dma("sp", bf[:, 4:5], bcols(512, 1), (), (kb + (1,),), CH, True, True)
        dma("sp", bf[0:64, 5:6], bl[576:640].rearrange("(t p) -> p t", p=64), (), (kb + (2,),), CH, True, True)
        dma("sp", bf[64:128, 5:6], bl[512:576].rearrange("(t p) -> p t", p=64), (), (kb + (3,),), CH, True, True)
        dma("sp", bf[:, 6:10], bcols(1280, 4), (), (kb + (4,),), CH, True, True)
        dma("sp", bf[:, 10:14], bcols(768, 4), (), (kb + (5,),), CH, True, True)
        dma("sp", bf[:, 14:18], bcols(2304, 4), (), (kb + (6,),), CH, True, True)
        dma("sp", bf[:, 18:34], bcols(2816, 16), (), (kb + (7,),), CH, True, True)
        bkeys = tuple(kb + (i,) for i in range(8))
        P.add("dve", (lambda l: lambda e: e.tensor_scalar(out=halfb[l][:], in0=bias_fm[l][:, 18:34], scalar1=0.5,
                                                          scalar2=None, op0=ALU.mult))(l),
              bkeys, (("halfb", l),))
        br = brow[l]
        bl2 = b_in[l:l + 1, :]
        rk = ("brow", l)
        for i, (d0, s0, n) in enumerate(((0, 640, 64), (64, 640, 64), (128, 704, 64), (192, 704, 64), (256, 1792, 512))):
            dma("pool", br[0:1, d0:d0 + n], bl2[:, s0:s0 + n], (), (rk + (i,),), CH, True)
        dma("pool", br[0:1, 768:1792], b_out[l:l + 1, :], (), (rk + (5,),), CH, True)
        dma("sp", vng[l][:], vn_g[l].rearrange("(g p) -> p g", p=128), (), (("vng", l),), CH, True, True)
        for hk in range(2):
            for g in range(4):
                h = hk * 4 + g
                P.add("dve", (lambda l, hk, g, h: lambda e: e.tensor_scalar(
                    out=sinkrow[0:1, l * 2 + hk, g * 128:(g + 1) * 128], in0=ones_f[0:1, :],
                    scalar1=sinkexp[0:1, l * 8 + h:l * 8 + h + 1], scalar2=None, op0=ALU.mult))(l, hk, g, h),
                    (("sinkexp",), ("ones_f",)), (("sinkrow", l, hk, g),))
    for l in layers:
        cdma(wnat[:], w_s[l].rearrange("g t s -> t g s"), ("z", 0), chan="wl")
        bT = next_bank()
        for g in range(4):
            P.add("pe", (lambda g, bT: lambda e: e.transpose(out=ps[bT][:, g * 128:(g + 1) * 128], in_=wnat[:, g, :],
                                                             identity=ident_f[:]))(g, bT),
                  (("z", 0), ("ident_f",)), (("ps", bT),))
        P.add("dve", (lambda bT: lambda e: e.tensor_copy(out=wT_f[:].rearrange("p g t -> p (g t)"), in_=ps[bT][:]))(bT),
              (("ps", bT),), (("z", 0),))
        P.add("pool", lambda e: e.affine_select(out=wT_f[:], in_=wT_f[:], pattern=[[0, 4], [1, 128]],
                                                 compare_op=ALU.is_ge, fill=0.0, base=0, channel_multiplier=-1),
              (("z", 0),), (("z", 0),))
        P.add("dve", (lambda l: lambda e: e.tensor_copy(out=wT_bf[l][:], in_=wT_f[:]))(l), (("z", 0),), (("wT_bf", l),))
        bR = next_bank()
        P.add("pe", (lambda bR: lambda e: e.matmul(ps[bR][0:1, :], ones_f[:, 0:1], wT_f[:].rearrange("p g t -> p (g t)"),
                                                   start=True, stop=True))(bR),
              (("z", 0), ("ones_f",)), (("ps", bR),))
        P.add("dve", (lambda bR: lambda e: e.tensor_copy(out=R2[0:1, :], in_=ps[bR][0:1, :]))(bR),
              (("ps", bR),), (("z", 1),))
        cdma(R2[1:2, :], b_s[l:l + 1, :], ("z", 1), chan="r2l")
        cdma(L2[0:1, :], vn_b[l:l + 1, :], ("z", 1), reads=(("z", 1),), chan="l2l")
        bC = next_bank()
        for g in range(4):
            P.add("pe", (lambda g, bC: lambda e: e.matmul(ps[bC][:, g * 128:(g + 1) * 128], L2[0:2, g * 128:(g + 1) * 128],
                                                          R2[0:2, g * 128:(g + 1) * 128], start=True, stop=True))(g, bC),
                  (("z", 1), ("z", 1), ("z", 1)), (("ps", bC),))
        P.add("dve", (lambda l, bC: lambda e: e.tensor_copy(out=Cc[l][:].rearrange("p g t -> p (g t)"), in_=ps[bC][:]))(l, bC),
              (("ps", bC),), (("Cc", l),))

    for l in layers:
        setup_cvt(l)

    slab_seq = [(t, l, s) for t in range(NT) for l in layers for s in range(14)]
    wstate = {"next_load": 0, "cur": 0}

    def issue_load():
        n = wstate["next_load"]
        if n >= len(slab_seq):
            return
        wstate["next_load"] = n + 1
        _, l, s = slab_seq[n]
        slot = n % R_SLOTS
        dma("sp", ring[:, slot, :], ws[l][s], tuple(cvt_keys[(l, s)]), (("ring", slot),), chan=("ring", slot))

    def acquire(l, s):
        n = wstate["cur"]
        assert slab_seq[n][1] == l and slab_seq[n][2] == s, (slab_seq[n], l, s)
        slot = n % R_SLOTS
        return slot

    def release():
        wstate["cur"] += 1
        issue_load()

    def slab8(slot):
        return ring[:, slot, :].rearrange("p (k c) -> p k c", k=8)

    def slab4(slot):
        return ring[:, slot, :].rearrange("p (k c) -> p k c", k=4)

    XT_ALL = tuple(("xT", j) for j in range(4))

    def ln_p1(src, src_key, eps):
        si = rot("stat", 8)
        st = stats[:, si, :]
        skey = ("stat", si)
        P.add("dve", lambda e: e.bn_stats(out=st[:, 0:6], in_=src[:, 0:512]), (src_key,), (skey,))
        P.add("dve", lambda e: e.bn_stats(out=st[:, 6:12], in_=src[:, 512:1024]), (src_key, skey), (skey,))
        P.add("dve", lambda e: e.bn_aggr(out=mv[:, si, :], in_=st), (skey,), (("mv", si),))
        P.add("dve", lambda e: e.tensor_scalar(out=sm[:, si, 0:1], in0=mv[:, si, 1:2], scalar1=eps, scalar2=None,
                                               op0=ALU.add), (("mv", si),), (("sm0", si),))
        P.add("pool", lambda e: e.tensor_tensor(out=sm[:, si, 1:2], in0=sm[:, si, 0:1], in1=neghalf[:], op=ALU.pow),
              (("sm0", si), ("neghalf",)), (("sm1", si),))
        return si

    deferred_b = []
    deferred_c = []

    def flush_deferred(c=True):
        while deferred_b:
            deferred_b.pop(0)()
        while c and deferred_c:
            deferred_c.pop(0)()

    def ln_p2(si, src, src_key, dst, dst_key, gi, make_bf, after=None):
        P.add("dve", lambda e: e.tensor_scalar(out=sm[:, si, 2:3], in0=mv[:, si, 0:1], scalar1=-1.0,
                                               scalar2=sm[:, si, 1:2], op0=ALU.mult, op1=ALU.mult),
              (("mv", si), ("sm1", si)), (("sm2", si),))
        SK = (("sm1", si), ("sm2", si))
        if make_bf:
            j = make_bf - 1
            bi = rot("xbf", 4)
            act(xbf[:, bi, :], src, AF.Identity, (src_key,) + SK, (("xbf", bi),),
                bias=sm[:, si, 2:3], scale=sm[:, si, 1:2])

            def part_b():
                for kc in range(8):
                    P.add("pe", (lambda kc, bi: lambda e: e.transpose(out=psT[:, kc * 128:(kc + 1) * 128],
                                                                     in_=xbf[:, bi, kc * 128:(kc + 1) * 128],
                                                                     identity=ident_bf[:]))(kc, bi),
                          (("xbf", bi), ("ident_bf",)), (("psT",),))
                xo = xT[:, :, j * 128:(j + 1) * 128]
                P.add("dve", lambda e: e.tensor_tensor(out=xo, in0=psT[:].rearrange("p (k t) -> p k t", k=8),
                                                       in1=gT[:, gi, :].unsqueeze(2).to_broadcast([128, 8, 128]),
                                                       op=ALU.mult),
                      (("psT",), ("gT", gi)), (("xT", j),))
                P.add("dve", lambda e: e.tensor_tensor(out=xo, in0=xo,
                                                       in1=gT[:, gi + 1, :].unsqueeze(2).to_broadcast([128, 8, 128]),
                                                       op=ALU.add),
                      (("xT", j), ("gT", gi + 1)), (("xT", j),))
            deferred_b.append(part_b)
        act(dst, src, AF.Identity, (src_key,) + SK, (dst_key,), bias=sm[:, si, 2:3], scale=sm[:, si, 1:2])

        def part_c():
            P.add("pool", lambda e: e.tensor_tensor(out=dst, in0=dst, in1=gb[:, gi, :], op=ALU.mult),
                  (dst_key, ("gb", gi)), (dst_key,))
            P.add("pool", lambda e: e.tensor_tensor(out=dst, in0=dst, in1=gb[:, gi + 1, :], op=ALU.add),
                  (dst_key, ("gb", gi + 1)), (dst_key,))
            if after is not None:
                after()
        deferred_c.append(part_c)

    def to_xT(src, src_key, j):
        bi = rot("xbf", 4)
        act(xbf[:, bi, :], src, AF.Copy, (src_key,), (("xbf", bi),))
        for kc in range(8):
            P.add("pe", (lambda kc, bi: lambda e: e.transpose(out=psT[:, kc * 128:(kc + 1) * 128],
                                                             in_=xbf[:, bi, kc * 128:(kc + 1) * 128],
                                                             identity=ident_bf[:]))(kc, bi),
                  (("xbf", bi), ("ident_bf",)), (("psT",),))
        act(xT[:, :, j * 128:(j + 1) * 128], psT[:].rearrange("p (k t) -> p k t", k=8), AF.Copy,
            (("psT",),), (("xT", j),))

    lnin_si = {}

    def lnin_step(j):
        if j < 4:
            lnin_si[j] = ln_p1(xres[:, j, :], ("xres", j), EPS)
        if j >= 1:
            ln_p2(lnin_si[j - 1], xres[:, j - 1, :], ("xres", j - 1), xres[:, j - 1, :], ("xres", j - 1), 0, j)

    def layer(l, t, last):
        first_tile_of_seq = (t * 512) % seq_len == 0
        bfm = bias_fm[l]
        BK = tuple(("bias_fm", l, i) for i in range(8))

        def fm_tile(slot, col, out_fn, reads_extra=()):
            b = next_bank("A")
            sl = slab8(slot)
            for kc in range(8):
                mm(ps[b][:], sl[:, kc, col * 128:(col + 1) * 128], xT[:, kc, :], kc == 0, kc == 7,
                   (("ring", slot),) + XT_ALL, (("ps", b),))
            out_fn(b)

        slot = acquire(l, 0)
        for c in range(4):
            fm_tile(slot, c, (lambda c: lambda b: act(qT[:, c, :], ps[b][:], AF.Identity, (("ps", b),) + BK,
                                                      (("qT", c),), bias=bfm[:, c:c + 1]))(c))
        release()
        slot = acquire(l, 1)
        kp = kpad[l]

        def k_out(b, col, var_lo, var_hi):
            act(kp[0:64, var_lo, 128:640], ps[b][0:64, :], AF.Identity, (("ps", b), ("kpadz", l)) + BK,
                (("kpad", l, var_lo),), bias=bfm[0:64, col:col + 1])
            act(kp[64:128, var_hi, 128:640], ps[b][64:128, :], AF.Identity, (("ps", b), ("kpadz", l)) + BK,
                (("kpad", l, var_hi),), bias=bfm[64:128, col:col + 1])
        fm_tile(slot, 0, lambda b: k_out(b, 4, 0, 3))
        fm_tile(slot, 1, lambda b: k_out(b, 5, 2, 1))
        sl = slab8(slot)
        RK = tuple(("brow", l, i) for i in range(6))
        for j in range(4):
            b = next_bank()
            for kc in range(8):
                mm(ps[b][:, 0:256], xT[:, kc, j * 128:(j + 1) * 128], sl[:, kc, 256:512], kc == 0, False,
                   (("ring", slot), ("xT", j)), (("ps", b),))
            mm(ps[b][:, 0:256], onesrow[0:1, :], brow[l][0:1, 0:256], False, True, RK + (("onesrow",),), (("ps", b),))
            act(vdup[l][:, 1 + j, :], ps[b][:, 0:256], AF.Copy, (("ps", b),), (("vdup", l, 1 + j),))
        release()
        flush_deferred()
        slot = acquire(l, 2)
        sl = slab8(slot)
        for j in range(4):
            b = next_bank()
            for kc in range(8):
                mm(ps[b][:], xT[:, kc, j * 128:(j + 1) * 128], sl[:, kc, :], kc == 0, False,
                   (("ring", slot), ("xT", j)), (("ps", b),))
            mm(ps[b][:], onesrow[0:1, :], brow[l][0:1, 256:768], False, True, RK + (("onesrow",),), (("ps", b),))
            vi = rot("vbg", 2)
            act(vbg[:, vi, :], ps[b][:], AF.Gelu_apprx_tanh, (("ps", b),), (("vbg", vi),))
            si = rot("stat", 8)
            P.add("dve", (lambda vi, si: lambda e: e.bn_stats(out=stats[:, si, 0:6], in_=vbg[:, vi, :]))(vi, si),
                  (("vbg", vi),), (("stat", si),))
            P.add("dve", (lambda si: lambda e: e.bn_aggr(out=mv[:, si, :], in_=stats[:, si, 0:6]))(si),
                  (("stat", si),), (("mv", si),))
            P.add("dve", (lambda si: lambda e: e.tensor_scalar(out=sm[:, si, 0:1], in0=mv[:, si, 1:2], scalar1=EPS,
                                                               scalar2=None, op0=ALU.add))(si),
                  (("mv", si),), (("sm0", si),))
            P.add("pool", (lambda si: lambda e: e.tensor_tensor(out=sm[:, si, 1:2], in0=sm[:, si, 0:1], in1=neghalf[:],
                                                                op=ALU.pow))(si),
                  (("sm0", si), ("neghalf",)), (("sm1", si),))
            P.add("dve", (lambda vi, si, j: lambda e: e.tensor_scalar(out=vnh[:, j, :], in0=vbg[:, vi, :],
                                                                      scalar1=mv[:, si, 0:1], scalar2=sm[:, si, 1:2],
                                                                      op0=ALU.subtract, op1=ALU.mult))(vi, si, j),
                  (("vbg", vi), ("mv", si), ("sm1", si)), (("vnh", j),))
        release()
        slot = acquire(l, 3)
        for c in range(4):
            fm_tile(slot, c, (lambda c: lambda b: act(uT[:, c, :], ps[b][:], AF.Gelu_apprx_tanh, (("ps", b),) + BK,
                                                      (("uT", c),), bias=bfm[:, 6 + c:7 + c]))(c))
        release()
        slot = acquire(l, 4)
        for c in range(4):
            fm_tile(slot, c, (lambda c: lambda b: act(gaT[:, c, :], ps[b][:], AF.Silu, (("ps", b),) + BK,
                                                      (("gaT", c),), bias=bfm[:, 10 + c:11 + c]))(c))
        release()
        slot = acquire(l, 5)
        for c in range(4):
            fm_tile(slot, c, (lambda c: lambda b: act(gbT[:, c, :], ps[b][:], AF.Silu, (("ps", b),) + BK,
                                                      (("gbT", c),) + tuple(("ybT", jj) for jj in range(4)),
                                                      bias=bfm[:, 14 + c:15 + c]))(c))
        release()
        for c in range(4):
            P.add("pool", (lambda c: lambda e: e.tensor_tensor(out=uT[:, c, :], in0=uT[:, c, :], in1=gbT[:, c, :],
                                                               op=ALU.mult))(c),
                  (("uT", c), ("gbT", c)), (("uT", c),))

        def attn_scores(j, hk):
            first = first_tile_of_seq and j == 0
            kbs = [1] if first else [0, 1]
            pts = []
            for kbi in kbs:
                cb = j + kbi
                bS = next_bank("B")
                for g in range(4):
                    var = hk * 2 + (g % 2)
                    mm(ps[bS][:, g * 128:(g + 1) * 128], kp[:, var, cb * 128:(cb + 1) * 128],
                       qT[:, 2 * hk + g // 2, j * 128:(j + 1) * 128], True, True,
                       (("kpad", l, var), ("qT", 2 * hk + g // 2), ("kpadz", l)), (("ps", bS),))
                pi = rot("PT", 4)
                act(PT[:, pi, :], ps[bS][:], AF.Exp, (("ps", bS),), (("PT", pi),), scale=0.125)
                P.add("dve", (lambda pi, kbi: lambda e: e.tensor_tensor(out=PT[:, pi, :], in0=PT[:, pi, :],
                                                                       in1=maskc[:, kbi, :], op=ALU.mult))(pi, kbi),
                      (("PT", pi), ("maskc",)), (("PT", pi),))
                pts.append((pi, cb))
            return pts

        def attn_pv(j, hk, pts):
            def gsel(ap, b):
                return ap.rearrange("p (a b t) -> p a b t", a=2, b=2)[:, :, b, :]
            bO = next_bank("C")
            for b in range(2):
                for i, (pi, cb) in enumerate(pts):
                    mm(ps[bO][b * 64:(b + 1) * 64, 0:256].rearrange("p (a t) -> p a t", a=2),
                       vdup[l][:, cb, hk * 128:hk * 128 + 64], gsel(PT[:, pi, :], b), i == 0, i == len(pts) - 1,
                       (("vdup", l, cb), ("PT", pi)), (("ps", bO),))
            bD = next_bank("C")
            for b in range(2):
                for i, (pi, cb) in enumerate(pts):
                    mm(ps[bD][b * 64:(b + 1) * 64, 0:256].rearrange("p (a t) -> p a t", a=2),
                       ones_bf[:, 0:64], gsel(PT[:, pi, :], b), i == 0, False,
                       (("ones_bf",), ("PT", pi)), (("ps", bD),))
                mm(ps[bD][b * 64:(b + 1) * 64, 0:256].rearrange("p (a t) -> p a t", a=2),
                   onesrow[0:1, 0:64], gsel(sinkrow[0:1, l * 2 + hk, :], b), False, True,
                   tuple(("sinkrow", l, hk, g) for g in range(4)) + (("onesrow",),), (("ps", bD),))
            ri = rot("rec", 1)
            P.add("dve", (lambda ri, bD: lambda e: e.reciprocal(out=rec[:, ri, 0:256], in_=ps[bD][:, 0:256]))(ri, bD),
                  (("ps", bD),), (("rec", ri),))
            ti = rot("t1", 2)
            P.add("dve", (lambda ti, ri, bO: lambda e: e.tensor_tensor(
                out=t1[:, ti, :], in0=ps[bO][:, 0:256], in1=rec[:, ri, 0:256], op=ALU.mult))(ti, ri, bO),
                (("ps", bO), ("rec", ri)), (("t1", ti),))
            P.add("dve", (lambda ti, hk, j: lambda e: e.tensor_tensor(
                out=yaT[:, 2 * hk:2 * hk + 2, j * 128:(j + 1) * 128],
                in0=t1[:, ti, :].rearrange("p (a t) -> p a t", a=2),
                in1=gaT[:, 2 * hk:2 * hk + 2, j * 128:(j + 1) * 128], op=ALU.mult))(ti, hk, j),
                (("t1", ti), ("gaT", 2 * hk), ("gaT", 2 * hk + 1)), (("yaT", j, hk),))

        def sgu_block(j):
            bM = next_bank("C")
            for g in range(4):
                mm(ps[bM][:, g * 128:(g + 1) * 128], vnh[:, j, g * 128:(g + 1) * 128], wT_bf[l][:, g, :], True, True,
                   (("vnh", j), ("wT_bf", l)), (("ps", bM),))
            mi = rot("mixed", 1)
            for g in range(4):
                P.add("dve", (lambda g, mi, bM: lambda e: e.scalar_tensor_tensor(
                    out=mixed[:, mi, g * 128:(g + 1) * 128], in0=ps[bM][:, g * 128:(g + 1) * 128],
                    scalar=vng[l][:, g:g + 1], in1=Cc[l][:, g, :], op0=ALU.mult, op1=ALU.add))(g, mi, bM),
                    (("ps", bM), ("vng", l), ("Cc", l), ("mixed", mi)), (("mixed", mi),))
            P.add("dve", (lambda mi, j: lambda e: e.tensor_tensor(
                out=ybT[:, :, j * 128:(j + 1) * 128], in0=mixed[:, mi, :].rearrange("p (g t) -> p g t", g=4),
                in1=uT[:, :, j * 128:(j + 1) * 128], op=ALU.mult))(mi, j),
                (("mixed", mi),) + tuple(("uT", c) for c in range(4)), (("ybT", j),) + tuple(("gbT", c) for c in range(4)))

        units = [(j, hk) for j in range(4) for hk in range(2)]
        pend = {}
        ti_ = 0
        for i in range(4):
            slot = acquire(l, 6 + i)
            dst = tRa if i < 2 else tRb
            for c in range(4):
                m = (i % 2) * 4 + c
                col = (i * 4 + c)
                fm_tile(slot, c, (lambda dst, m, col: lambda b: act(dst[:, m, :], ps[b][:], AF.Tanh,
                                                                    (("ps", b), ("halfb", l)),
                                                                    ((dst is tRa and "tRa" or "tRb", m),),
                                                                    bias=halfb[l][:, col:col + 1], scale=0.5))(dst, m, col))
                if ti_ % 2 == 0:
                    u = ti_ // 2
                    if u - 1 in pend:
                        attn_pv(units[u - 1][0], units[u - 1][1], pend.pop(u - 1))
                    pend[u] = attn_scores(*units[u])
                elif ti_ // 2 < 4:
                    sgu_block(ti_ // 2)
                ti_ += 1
            release()
        attn_pv(units[7][0], units[7][1], pend.pop(7))
        P.add("pool", lambda e: e.tensor_copy(out=kp[:, :, 0:128], in_=kp[:, :, 512:640]),
              tuple(("kpad", l, v) for v in range(4)), tuple(("kpad", l, v) for v in range(4)))
        P.add("pool", lambda e: e.tensor_copy(out=vdup[l][:, 0, :], in_=vdup[l][:, 4, :]),
              (("vdup", l, 4),), (("vdup", l, 0),))

        sa = acquire(l, 10)
        sA = slab4(sa)
        sbn = (wstate["cur"] + 1) % R_SLOTS
        YA = tuple(("yaT", j, hk) for j in range(4) for hk in range(2))
        YB = tuple(("ybT", j) for j in range(4))
        sB = slab4(sbn)
        bBs = {}

        def emit_B(m):
            bB = next_bank()
            bBs[m] = bB
            for kc in range(4):
                mm(ps[bB][:], sB[:, kc, m * 128:(m + 1) * 128], ybT[:, kc, :], kc == 0, kc == 3,
                   (("ring", sbn),) + YB, (("ps", bB),))

        def emit_A(m):
            bA = next_bank()
            for kc in range(4):
                mm(ps[bA][:], sA[:, kc, m * 128:(m + 1) * 128], yaT[:, kc, :], kc == 0, kc == 3,
                   (("ring", sa),) + YA, (("ps", bA),))
            bB = bBs[m]
            ai = rot("Ap", 2)
            P.add("dve", (lambda ai, bB, m: lambda e: e.scalar_tensor_tensor(
                out=Bp[:, ai, :], in0=tRb[:, m, :], scalar=1.0, in1=ps[bB][:], op0=ALU.add, op1=ALU.mult))(ai, bB, m),
                (("tRb", m), ("ps", bB)), (("Bp", ai),))
            P.add("dve", (lambda ai, bA, m: lambda e: e.scalar_tensor_tensor(
                out=Ap[:, ai, :], in0=tRa[:, m, :], scalar=1.0, in1=ps[bA][:], op0=ALU.add, op1=ALU.mult))(ai, bA, m),
                (("tRa", m), ("ps", bA)), (("Ap", ai),))
            if m == 7:
                P.add("dve", (lambda ai, m: lambda e: e.tensor_tensor(out=mT[:, m, :], in0=Ap[:, ai, :],
                                                                      in1=Bp[:, ai, :], op=ALU.add))(ai, m),
                      (("Ap", ai), ("Bp", ai)), (("mT", m),))
            else:
                P.add("pool", (lambda ai, m: lambda e: e.tensor_tensor(out=mT[:, m, :], in0=Ap[:, ai, :],
                                                                       in1=Bp[:, ai, :], op=ALU.add))(ai, m),
                      (("Ap", ai), ("Bp", ai)), (("mT", m),))

        emit_B(0)
        for m in range(8):
            if m + 1 < 8:
                emit_B(m + 1)
            emit_A(m)
        release()
        s2 = acquire(l, 11)
        assert s2 == sbn
        release()
        def ln_block2(j, zi, si):
            if last:
                r0 = t * 512 + j * 128

                def out_dma():
                    dma("sp", y_d[r0:r0 + 128, :], zbuf[:, zi, :], (("z", zi),), (("y", t, j),), chan=("out", zi))
                ln_p2(si, zbuf[:, zi, :], ("z", zi), zbuf[:, zi, :], ("z", zi), 2 + 2 * l, 0, after=out_dma)
            else:
                ln_p2(si, zbuf[:, zi, :], ("z", zi), xres[:, j, :], ("xres", j), 2 + 2 * l, j + 1)

        so0 = acquire(l, 12)
        so1 = (wstate["cur"] + 1) % R_SLOTS
        MT = tuple(("mT", m) for m in range(8))
        zis = []
        sis = []
        for j in range(4):
            zi = rot("z", 4)
            zis.append(zi)
            for h, so in ((0, so0), (1, so1)):
                b = next_bank()
                sl = slab8(so)
                for kc in range(8):
                    mm(ps[b][:], mT[:, kc, j * 128:(j + 1) * 128], sl[:, kc, :], kc == 0, False,
                       (("ring", so),) + MT, (("ps", b),))
                mm(ps[b][:], twosrow[0:1, :], brow[l][0:1, 768 + h * 512:768 + (h + 1) * 512], False, True,
                   RK + (("twosrow",),), (("ps", b),))
                P.add("dve", (lambda zi, h, b, j: lambda e: e.scalar_tensor_tensor(
                    out=zbuf[:, zi, h * 512:(h + 1) * 512], in0=xres[:, j, h * 512:(h + 1) * 512], scalar=2.0 * ALPHA,
                    in1=ps[b][:], op0=ALU.mult, op1=ALU.add))(zi, h, b, j),
                    (("xres", j), ("ps", b), ("z", zi)), (("z", zi),))
            nxt = last and t + 1 < NT
            if nxt:
                load_x(t + 1, j)
            sis.append(ln_p1(zbuf[:, zi, :], ("z", zi), 4.0 * EPS))
            if j >= 1:
                ln_block2(j - 1, zis[j - 1], sis[j - 1])
            if not last and j >= 2:
                deferred_b.pop(0)()
            if nxt and do_ln_in and j >= 1:
                lnin_step(j - 1)
        ln_block2(3, zis[3], sis[3])
        if last and t + 1 < NT and do_ln_in:
            lnin_step(3)
            lnin_step(4)
        elif last and t + 1 < NT:
            for jj in range(4):
                to_xT(xres[:, jj, :], ("xres", jj), jj)
        flush_deferred(c=not (last and t + 1 < NT))
        release()
        s3 = acquire(l, 13)
        assert s3 == so1
        release()

    for _ in range(R_SLOTS):
        issue_load()
    for t in range(NT):
        if t == 0:
            if do_ln_in:
                for jj in range(5):
                    lnin_step(jj)
            else:
                for jj in range(4):
                    to_xT(xres[:, jj, :], ("xres", jj), jj)
            flush_deferred()
        for li, l in enumerate(layers):
            layer(l, t, li == len(layers) - 1)

    flush_deferred()
    P.emit(nc, es)
    es.close()
    return nc


_CACHE = {}


def _get_nc(n_tok, seq_len, layers, do_ln_in):
    key = (n_tok, seq_len, tuple(layers), do_ln_in)
    if key not in _CACHE:
        _CACHE[key] = build(n_tok, seq_len, layers, do_ln_in)
    return _CACHE[key]


def _run(x, params, layers=(0, 1), do_ln_in=True, n_cores=N_CORES):
    B, S, _ = x.shape
    per = B // n_cores
    n_tok = per * S
    nc = _get_nc(n_tok, S, layers, do_ln_in)
    f = lambda a: np.ascontiguousarray(np.asarray(a, dtype=np.float32))
    shared = {
        "ln_in_g": f(params["ln_in_g"]).reshape(1, D), "ln_in_b": f(params["ln_in_b"]).reshape(1, D),
        "w_in": f(params["w_in"]), "b_in": f(params["b_in"]), "sinks": f(params["sinks"]).reshape(1, 16),
        "vn_g": f(params["vn_g"]), "vn_b": f(params["vn_b"]), "w_s": f(params["w_s"]),
        "b_s": f(params["b_s"]).reshape(2, 512), "p_a": f(params["p_a"]), "p_b": f(params["p_b"]),
        "w_out": f(params["w_out"]), "b_out": f(params["b_out"]), "ln_g": f(params["ln_g"]), "ln_b": f(params["ln_b"]),
    }
    xf = f(x).reshape(n_cores, n_tok, D)
    in_maps = [dict(shared, x=xf[c]) for c in range(n_cores)]
    res = run_bass_kernel_spmd(nc, in_maps, core_ids=list(range(n_cores)))
    out = np.stack([np.asarray(r["y"]) for r in res.results], axis=0)
    return out.reshape(B, S, D).astype(np.float32)


def kernel(x, ln_in_g, ln_in_b, w_in, b_in, sinks, vn_g, vn_b, w_s, b_s, p_a, p_b, w_out, b_out, ln_g, ln_b):
    params = dict(ln_in_g=ln_in_g, ln_in_b=ln_in_b, w_in=w_in, b_in=b_in, sinks=sinks, vn_g=vn_g, vn_b=vn_b,
                  w_s=w_s, b_s=b_s, p_a=p_a, p_b=p_b, w_out=w_out, b_out=b_out, ln_g=ln_g, ln_b=ln_b)
    return _run(np.asarray(x), params)
```
